# Optimizing a Trainium2 kernel written in Bass

```python
import math
import jax, jax.numpy as jnp
from jax import lax
import numpy as np

D_MODEL = 1024
BATCH = 4
SEQ = 8192
DEPTH = 4

D_FF = 2816
DA_HEADS = 8
DA_HEAD_DIM = 64
DA_QK_WIDTH = DA_HEADS * 2 * DA_HEAD_DIM
DA_V_WIDTH = DA_HEADS * 2 * DA_HEAD_DIM
Q_BLOCK = 128
REL_BUCKETS = 32
REL_MAX_DIST = 128
CONV_CH = D_MODEL
CONV_WIDTH = 31
GD_HEADS = 8
GD_DK = 128
GD_DV = 128
GD_KW = GD_HEADS * GD_DK
GD_VW = GD_HEADS * GD_DV
GD_CONV_WIDTH = 5
GD_CHUNK = 64
N_DIR = 2
EPS = 1e-6

AC_IN = 2 * DA_QK_WIDTH + DA_V_WIDTH + 2 * CONV_CH
AC_OUT = DA_V_WIDTH + CONV_CH
GD_IN = 2 * GD_KW + 2 * GD_VW + 2 * N_DIR * GD_HEADS
N_AC_LAYERS = (DEPTH + 1) // 2
N_GD_LAYERS = DEPTH // 2

kernel_name = 'hybrid_diffattn_conformer_gdn_encoder'


def rms_norm(x, w):
    xf = x.astype(jnp.float32)
    y = xf * lax.rsqrt(jnp.mean(xf * xf, axis=-1, keepdims=True) + EPS)
    return (y * w.astype(jnp.float32)).astype(x.dtype)


def layer_norm(x, w, b):
    xf = x.astype(jnp.float32)
    xc = xf - jnp.mean(xf, axis=-1, keepdims=True)
    y = xc * lax.rsqrt(jnp.mean(xc * xc, axis=-1, keepdims=True) + EPS)
    return (y * w.astype(jnp.float32) + b.astype(jnp.float32)).astype(x.dtype)


def l2_normalize(x):
    xf = x.astype(jnp.float32)
    return xf * lax.rsqrt(jnp.sum(xf * xf, axis=-1, keepdims=True) + EPS)


def swiglu_ffn(h, w_in, w_out):
    gate, up = jnp.split(h @ w_in, 2, axis=-1)
    return (jax.nn.silu(gate) * up) @ w_out


def depthwise_conv_centred(x, w):
    pad = w.shape[0] // 2
    return lax.conv_general_dilated(
        x, w[:, None, :].astype(x.dtype), window_strides=(1,), padding=((pad, pad),),
        dimension_numbers=('NWC', 'WIO', 'NWC'), feature_group_count=x.shape[-1])


def lambda_init_fn(layer_idx):
    return 0.8 - 0.6 * math.exp(-0.3 * layer_idx)


def relative_bucket(rel):
    half = REL_BUCKETS // 2
    max_exact = half // 2
    ret = jnp.where(rel > 0, half, 0)
    n = jnp.abs(rel)
    nf = jnp.maximum(n, 1).astype(jnp.float32)
    large = max_exact + (jnp.log(nf / max_exact) / math.log(REL_MAX_DIST / max_exact)
                         * (half - max_exact)).astype(jnp.int32)
    large = jnp.minimum(large, half - 1)
    return ret + jnp.where(n < max_exact, n, large)


def diff_attention(q, k, v, rel_bias, lam):
    b_, s_ = q.shape[0], q.shape[1]
    nb = s_ // Q_BLOCK
    qb = jnp.moveaxis(q.reshape(b_, nb, Q_BLOCK, DA_HEADS, 2, DA_HEAD_DIM), 1, 0)
    kpos = jnp.arange(s_)
    scale = DA_HEAD_DIM ** -0.5

    def block(args):
        i, q_i = args
        qpos = i * Q_BLOCK + jnp.arange(Q_BLOCK)
        bucket = relative_bucket(kpos[None, :] - qpos[:, None])
        bias = jnp.moveaxis(rel_bias[bucket], -1, 0).astype(jnp.float32)
        s = jnp.einsum('bqhmd,bkhmd->bhmqk', q_i, k).astype(jnp.float32) * scale
        p = jax.nn.softmax(s + bias[None, :, None], axis=-1)
        a = p[:, :, 0] - lam * p[:, :, 1]
        return jnp.einsum('bhqk,bkhe->bqhe', a.astype(v.dtype), v)

    o = lax.map(block, (jnp.arange(nb), qb))
    return jnp.moveaxis(o, 0, 1).reshape(b_, s_, DA_HEADS, 2 * DA_HEAD_DIM)


def conformer_conv_branch(glu_in, conv_w, conv_b, norm_w, norm_b):
    a, g = jnp.split(glu_in, 2, axis=-1)
    u = a * jax.nn.sigmoid(g)
    u = depthwise_conv_centred(u, conv_w) + conv_b
    return jax.nn.silu(layer_norm(u, norm_w, norm_b))


def attn_conv_mixer(h, w_in, w_out, da_lambda, subln_w, conv_w, conv_b, conv_norm_w, conv_norm_b,
                    rel_bias, lambda_init):
    b_, s_, _ = h.shape
    proj = h @ w_in
    q = proj[..., :DA_QK_WIDTH].reshape(b_, s_, DA_HEADS, 2, DA_HEAD_DIM)
    k = proj[..., DA_QK_WIDTH:2 * DA_QK_WIDTH].reshape(b_, s_, DA_HEADS, 2, DA_HEAD_DIM)
    v = proj[..., 2 * DA_QK_WIDTH:2 * DA_QK_WIDTH + DA_V_WIDTH].reshape(b_, s_, DA_HEADS, 2 * DA_HEAD_DIM)
    glu_in = proj[..., 2 * DA_QK_WIDTH + DA_V_WIDTH:]
    lf = da_lambda.astype(jnp.float32)
    lam = jnp.exp(jnp.sum(lf[0] * lf[1])) - jnp.exp(jnp.sum(lf[2] * lf[3])) + lambda_init
    o_a = diff_attention(q, k, v, rel_bias, lam)
    o_a = (rms_norm(o_a, subln_w) * (1.0 - lambda_init)).reshape(b_, s_, DA_V_WIDTH)
    o_b = conformer_conv_branch(glu_in, conv_w, conv_b, conv_norm_w, conv_norm_b)
    return jnp.concatenate([o_a, o_b], axis=-1) @ w_out


def gated_delta_chunked(q, k, v, g, beta):
    b_, s_, hh, dk = q.shape
    dv = v.shape[-1]
    n = s_ // GD_CHUNK

    def to_chunks(t):
        t = t.astype(jnp.float32).reshape(b_, n, GD_CHUNK, hh, *t.shape[3:])
        return jnp.moveaxis(t, 3, 1)

    q, k, v, g, beta = map(to_chunks, (q, k, v, g, beta))
    g = jnp.cumsum(g, axis=-1)
    idx = jnp.arange(GD_CHUNK)
    incl = idx[:, None] >= idx[None, :]
    strict = idx[:, None] > idx[None, :]
    decay = jnp.exp(jnp.where(incl, g[..., :, None] - g[..., None, :], -jnp.inf))
    k_beta = k * beta[..., None]
    lmat = jnp.where(strict, jnp.einsum('bhncd,bhnjd->bhncj', k_beta, k) * decay, 0.0)
    rhs = jnp.concatenate([v * beta[..., None], k_beta * jnp.exp(g)[..., None]], axis=-1)
    sol = lax.linalg.triangular_solve(lmat, rhs, left_side=True, lower=True, unit_diagonal=True)
    u, w = sol[..., :dv], sol[..., dv:]
    qk = jnp.einsum('bhncd,bhnjd->bhncj', q, k) * decay
    q_g = q * jnp.exp(g)[..., None]
    k_g = k * jnp.exp(g[..., -1:] - g)[..., None]
    g_last = jnp.exp(g[..., -1])
    xs = tuple(jnp.moveaxis(t, 2, 0) for t in (u, w, qk, q_g, k_g, g_last))

    def step(state, inp):
        u_i, w_i, qk_i, qg_i, kg_i, gl_i = inp
        v_new = u_i - jnp.einsum('bhcd,bhde->bhce', w_i, state)
        o_i = jnp.einsum('bhcd,bhde->bhce', qg_i, state) + jnp.einsum('bhcj,bhje->bhce', qk_i, v_new)
        state = state * gl_i[..., None, None] + jnp.einsum('bhcd,bhce->bhde', kg_i, v_new)
        return state, o_i

    s0 = jnp.zeros((b_, hh, dk, dv), jnp.float32)
    _, o = lax.scan(step, s0, xs)
    o = jnp.moveaxis(o, 0, 2).reshape(b_, hh, s_, dv)
    return jnp.moveaxis(o, 1, 2)


def gated_deltanet_mixer(h, w_in, w_out, conv_w, a_log, dt_bias, norm_w):
    b_, s_, _ = h.shape
    proj = h @ w_in
    qkv = jax.nn.silu(depthwise_conv_centred(proj[..., :2 * GD_KW + GD_VW], conv_w))
    z = proj[..., 2 * GD_KW + GD_VW:2 * GD_KW + 2 * GD_VW].reshape(b_, s_, GD_HEADS, GD_DV)
    off = 2 * GD_KW + 2 * GD_VW
    b_logit = proj[..., off:off + N_DIR * GD_HEADS].reshape(b_, s_, N_DIR, GD_HEADS)
    a_in = proj[..., off + N_DIR * GD_HEADS:].reshape(b_, s_, N_DIR, GD_HEADS)
    q = l2_normalize(qkv[..., :GD_KW].reshape(b_, s_, GD_HEADS, GD_DK)) * (GD_DK ** -0.5)
    k = l2_normalize(qkv[..., GD_KW:2 * GD_KW].reshape(b_, s_, GD_HEADS, GD_DK))
    v = qkv[..., 2 * GD_KW:].reshape(b_, s_, GD_HEADS, GD_DV)
    beta = jax.nn.sigmoid(b_logit.astype(jnp.float32))
    g = -jnp.exp(a_log.astype(jnp.float32)) * jax.nn.softplus(
        a_in.astype(jnp.float32) + dt_bias.astype(jnp.float32))
    o_fwd = gated_delta_chunked(q, k, v, g[:, :, 0], beta[:, :, 0])
    flip = lambda t: jnp.flip(t, axis=1)
    o_bwd = flip(gated_delta_chunked(flip(q), flip(k), flip(v), flip(g[:, :, 1]), flip(beta[:, :, 1])))
    o = rms_norm((o_fwd + o_bwd).astype(h.dtype), norm_w) * jax.nn.silu(z)
    return o.reshape(b_, s_, GD_VW) @ w_out


def setup_inputs(seed: int = 0) -> dict:
    key = jax.random.key(seed)
    ks = jax.random.split(key, 24)
    f32 = jnp.float32

    def normal(k, shape, scale):
        return jax.random.normal(k, shape, f32) * scale

    x = normal(ks[0], (BATCH, SEQ, D_MODEL), 1.0)
    rel_bias = normal(ks[1], (REL_BUCKETS, DA_HEADS), 0.5)
    norm_w = 1.0 + normal(ks[2], (DEPTH, 3, D_MODEL), 0.02)
    final_norm_w = 1.0 + normal(ks[3], (D_MODEL,), 0.02)
    ffn_w_in = normal(ks[4], (DEPTH, 2, D_MODEL, 2 * D_FF), D_MODEL ** -0.5)
    ffn_w_out = normal(ks[5], (DEPTH, 2, D_FF, D_MODEL), D_FF ** -0.5)
    ac_w_in = normal(ks[6], (N_AC_LAYERS, D_MODEL, AC_IN), D_MODEL ** -0.5)
    ac_w_out = normal(ks[7], (N_AC_LAYERS, AC_OUT, D_MODEL), AC_OUT ** -0.5)
    da_lambda = normal(ks[8], (N_AC_LAYERS, 4, DA_HEAD_DIM), 0.1)
    da_subln_w = 1.0 + normal(ks[9], (N_AC_LAYERS, 2 * DA_HEAD_DIM), 0.02)
    conv_w = normal(ks[10], (N_AC_LAYERS, CONV_WIDTH, CONV_CH), CONV_WIDTH ** -0.5)
    conv_b = normal(ks[11], (N_AC_LAYERS, CONV_CH), 0.02)
    conv_norm_w = 1.0 + normal(ks[12], (N_AC_LAYERS, CONV_CH), 0.02)
    conv_norm_b = normal(ks[13], (N_AC_LAYERS, CONV_CH), 0.02)
    gd_w_in = normal(ks[14], (N_GD_LAYERS, D_MODEL, GD_IN), D_MODEL ** -0.5)
    gd_w_out = normal(ks[15], (N_GD_LAYERS, GD_VW, D_MODEL), GD_VW ** -0.5)
    gd_conv_w = normal(ks[16], (N_GD_LAYERS, GD_CONV_WIDTH, 2 * GD_KW + GD_VW), GD_CONV_WIDTH ** -0.5)
    gd_a_log = jnp.log(jax.random.uniform(ks[17], (N_GD_LAYERS, N_DIR, GD_HEADS), f32, 1.0, 16.0))
    dt = jnp.exp(jax.random.uniform(ks[18], (N_GD_LAYERS, N_DIR, GD_HEADS), f32,
                                    math.log(0.001), math.log(0.1)))
    gd_dt_bias = dt + jnp.log(-jnp.expm1(-dt))
    gd_norm_w = 1.0 + normal(ks[19], (N_GD_LAYERS, GD_DV), 0.02)
    return {'x': x, 'rel_bias': rel_bias, 'norm_w': norm_w, 'final_norm_w': final_norm_w,
            'ffn_w_in': ffn_w_in, 'ffn_w_out': ffn_w_out, 'ac_w_in': ac_w_in, 'ac_w_out': ac_w_out,
            'da_lambda': da_lambda, 'da_subln_w': da_subln_w, 'conv_w': conv_w, 'conv_b': conv_b,
            'conv_norm_w': conv_norm_w, 'conv_norm_b': conv_norm_b, 'gd_w_in': gd_w_in,
            'gd_w_out': gd_w_out, 'gd_conv_w': gd_conv_w, 'gd_a_log': gd_a_log,
            'gd_dt_bias': gd_dt_bias, 'gd_norm_w': gd_norm_w}


def reference(x, rel_bias, norm_w, final_norm_w, ffn_w_in, ffn_w_out, ac_w_in, ac_w_out,
              da_lambda, da_subln_w, conv_w, conv_b, conv_norm_w, conv_norm_b, gd_w_in,
              gd_w_out, gd_conv_w, gd_a_log, gd_dt_bias, gd_norm_w):
    for l in range(DEPTH):
        x = x + 0.5 * swiglu_ffn(rms_norm(x, norm_w[l, 0]), ffn_w_in[l, 0], ffn_w_out[l, 0])
        h = rms_norm(x, norm_w[l, 1])
        j = l // 2
        if l % 2 == 0:
            x = x + attn_conv_mixer(h, ac_w_in[j], ac_w_out[j], da_lambda[j], da_subln_w[j],
                                    conv_w[j], conv_b[j], conv_norm_w[j], conv_norm_b[j],
                                    rel_bias, lambda_init_fn(l))
        else:
            x = x + gated_deltanet_mixer(h, gd_w_in[j], gd_w_out[j], gd_conv_w[j], gd_a_log[j],
                                         gd_dt_bias[j], gd_norm_w[j])
        x = x + 0.5 * swiglu_ffn(rms_norm(x, norm_w[l, 2]), ffn_w_in[l, 1], ffn_w_out[l, 1])
    return rms_norm(x, final_norm_w)
```

```python
import contextlib
import types
import numpy as np
import concourse.bass as bass
import concourse.mybir as mybir
from concourse.bass_utils import run_bass_kernel_spmd

F32 = mybir.dt.float32
BF16 = mybir.dt.bfloat16
AF = mybir.ActivationFunctionType
ALU = mybir.AluOpType
AX = mybir.AxisListType

D = 1024
DFF = 2816
EPS = 1e-6


class Buf:
    __slots__ = ("w", "r")

    def __init__(self):
        self.w = None
        self.r = {}


def _freeze(fn):
    if fn.__closure__ is None:
        return fn
    cells = []
    for c in fn.__closure__:
        try:
            cells.append(types.CellType(c.cell_contents))
        except ValueError:
            cells.append(c)
    return types.FunctionType(fn.__code__, fn.__globals__, fn.__name__, fn.__defaults__, tuple(cells))


class Sched:
    def __init__(self, nc, n_dma_sems=24):
        self.nc = nc
        self.stack = contextlib.ExitStack()
        self.eng = {"pe": nc.tensor, "act": nc.scalar, "dve": nc.vector, "pool": nc.gpsimd, "sp": nc.sync}
        self.q = {e: [] for e in self.eng}
        self.sem = {}
        self.cnt = {}
        for e in self.eng:
            self.sem[e] = self.stack.enter_context(nc.semaphore("s_" + e))
            self.cnt[e] = 0
        self.dsem = {}
        for e in ("sp", "pool"):
            self.dsem[e] = [[self.stack.enter_context(nc.semaphore("d_%s%d" % (e, i))), 0]
                            for i in range(n_dma_sems)]
        self.dptr = {e: 0 for e in self.dsem}
        self.seen = {e: {} for e in self.eng}
        self.uid = 0
        self.out_toks = []
        self.nnames = 0

    def sb(self, shape, dt, name=None):
        self.nnames += 1
        return self.stack.enter_context(self.nc.sbuf_tensor(name or ("sb%d" % self.nnames), shape, dt))

    def ps(self, shape, dt, name=None):
        self.nnames += 1
        return self.stack.enter_context(self.nc.psum_tensor(name or ("ps%d" % self.nnames), shape, dt))

    def _need(self, eng, waits, tok):
        if tok is None:
            return
        _, sem, val = tok
        key = id(sem)
        if self.seen[eng].get(key, 0) >= val:
            return
        if key in waits and waits[key][1] >= val:
            return
        waits[key] = (sem, val)

    def op(self, eng, fn, reads=(), writes=(), dma=False, is_out=False, acc=False):
        waits = {}
        for b in reads:
            self._need(eng, waits, b.w)
        for b in writes:
            if b.w is not None and not (acc and b.w[0] == eng):
                self._need(eng, waits, b.w)
            for t in b.r.values():
                self._need(eng, waits, t)
        if dma:
            ring = self.dsem[eng]
            slot = ring[self.dptr[eng] % len(ring)]
            self.dptr[eng] += 1
            sem = slot[0]
            if slot[1] > 0:
                self._need(eng, waits, ("dma", sem, slot[1]))
            slot[1] += 16
            tok = ("dma", sem, slot[1])
            inc = (sem, 16)
        else:
            self.cnt[eng] += 1
            tok = (eng, self.sem[eng], self.cnt[eng])
            inc = (self.sem[eng], 1)
        for (sem, val) in waits.values():
            self.seen[eng][id(sem)] = val
        self.q[eng].append((list(waits.values()), _freeze(fn), inc))
        self.uid += 1
        rkey = eng if not dma else ("dma", self.uid)
        for b in reads:
            b.r[rkey] = tok
        for b in writes:
            b.w = tok
            b.r = {}
        if is_out:
            self.out_toks.append(tok)
        return tok

    def emit(self):
        nc = self.nc
        waits = {}
        for tok in self.out_toks:
            self._need("sp", waits, tok)
        final_waits = list(waits.values())
        q = self.q

        def run(name, e):
            for (ws, fn, inc) in q[name]:
                for (sem, val) in ws:
                    e.wait_ge(sem, val)
                fn(e).then_inc(inc[0], inc[1])
            if name == "sp":
                for (sem, val) in final_waits:
                    e.wait_ge(sem, val)

        with nc.Block() as block:
            @block.tensor
            def _(e):
                run("pe", e)

            @block.scalar
            def _(e):
                run("act", e)

            @block.vector
            def _(e):
                run("dve", e)

            @block.gpsimd
            def _(e):
                run("pool", e)

            @block.sync
            def _(e):
                run("sp", e)
        self.stack.close()


class Rot:
    def __init__(self, S, n, shape, dt, psum=False):
        self.items = []
        for _ in range(n):
            t = S.ps(shape, dt) if psum else S.sb(shape, dt)
            self.items.append((t, Buf()))
        self.i = 0

    def next(self):
        it = self.items[self.i % len(self.items)]
        self.i += 1
        return it


class Common:
    def __init__(self, S, ident_d):
        self.S = S
        self.idf = S.sb([128, 128], F32)
        self.Bidf = Buf()
        self.idb = S.sb([128, 128], BF16)
        self.Bidb = Buf()
        S.op("sp", lambda e: e.dma_start(out=self.idf[:], in_=ident_d), writes=[self.Bidf], dma=True)
        S.op("pool", lambda e: e.dma_start(out=self.idb[:], in_=ident_d), writes=[self.Bidb], dma=True)
        self._lazy = {}

    def _get(self, name, mk):
        if name not in self._lazy:
            self._lazy[name] = mk()
        return self._lazy[name]

    @property
    def junk(self):
        return self._get("junk", lambda: Rot(self.S, 2, [128, 1024], BF16))

    @property
    def stat(self):
        return self._get("stat", lambda: Rot(self.S, 4, [128, 4], F32))

    @property
    def hn(self):
        return self._get("hn", lambda: Rot(self.S, 2, [128, 1024], BF16))

    @property
    def ptr(self):
        return self._get("ptr", lambda: Rot(self.S, 2, [128, 1024], BF16, psum=True))


def rms_rstd(S, C, xt, Bx, rows, width=1024):
    jk, Bj = C.junk.next()
    st, Bs = C.stat.next()
    S.op("act", lambda e: e.activation(out=jk[0:rows, 0:width], in_=xt[0:rows, 0:width], func=AF.Square,
                                       accum_out=st[0:rows, 0:1]), reads=[Bx], writes=[Bj, Bs])
    S.op("act", lambda e: e.activation(out=st[0:rows, 1:2], in_=st[0:rows, 0:1], func=AF.Sqrt,
                                       bias=EPS, scale=1.0 / width), reads=[Bs], writes=[Bs])
    S.op("dve", lambda e: e.reciprocal(out=st[0:rows, 1:2], in_=st[0:rows, 1:2]), reads=[Bs], writes=[Bs])
    return st, Bs


def norm_transpose(S, C, xt, Bx, rows, hT, BhT, col0, nwT, BnwT):
    st, Bs = rms_rstd(S, C, xt, Bx, rows)
    hn, Bhn = C.hn.next()
    S.op("dve", lambda e: e.tensor_scalar_mul(out=hn[0:rows, :], in0=xt[0:rows, :], scalar1=st[0:rows, 1:2]),
         reads=[Bx, Bs], writes=[Bhn])
    ptr, Bptr = C.ptr.next()
    for k in range(8):
        S.op("pe", lambda e, k=k: e.transpose(out=ptr[:, k * 128:k * 128 + rows],
                                               in_=hn[0:rows, k * 128:(k + 1) * 128],
                                               identity=C.idb[0:rows, 0:rows]),
             reads=[Bhn, C.Bidb], writes=[Bptr], acc=True)
    for k in range(8):
        eng = "dve" if C.ptr.i % 2 == 0 else "act"
        if eng == "dve":
            S.op("dve", lambda e, k=k: e.tensor_scalar_mul(out=hT[:, k, col0:col0 + rows],
                                                           in0=ptr[:, k * 128:k * 128 + rows],
                                                           scalar1=nwT[:, k:k + 1]),
                 reads=[Bptr, BnwT], writes=[BhT[k]])
        else:
            S.op("act", lambda e, k=k: e.activation(out=hT[:, k, col0:col0 + rows],
                                                    in_=ptr[:, k * 128:k * 128 + rows],
                                                    func=AF.Identity, scale=nwT[:, k:k + 1]),
                 reads=[Bptr, BnwT], writes=[BhT[k]])


def phase_ffn(S, C, T, x_in, x_out, w_in_d, w_out_d, nwT_d, fin_d=None):
    NG = T // 512
    w1 = S.sb([128, 8, 2 * DFF], BF16)
    w2 = S.sb([128, 22, D], BF16)
    Bw1 = [Buf() for _ in range(8)]
    Bw2 = [Buf() for _ in range(2)]
    for k in range(8):
        S.op("pool", lambda e, k=k: e.dma_start(out=w1[:, k, :], in_=w_in_d[k * 128:(k + 1) * 128, :]),
             writes=[Bw1[k]], dma=True)
    w2v = w_out_d.rearrange("(f p) d -> p f d", p=128)
    for c in range(2):
        S.op("pool", lambda e, c=c: e.dma_start(out=w2[:, c * 11:(c + 1) * 11, :], in_=w2v[:, c * 11:(c + 1) * 11, :]),
             writes=[Bw2[c]], dma=True)
    nwT = S.sb([128, 8], F32)
    BnwT = Buf()
    S.op("sp", lambda e: e.dma_start(out=nwT[:], in_=nwT_d), writes=[BnwT], dma=True)
    if fin_d is not None:
        finb = S.sb([128, D], F32)
        Bfin = Buf()
        S.op("sp", lambda e: e.dma_start(out=finb[:], in_=fin_d), writes=[Bfin], dma=True)

    xin = Rot(S, 2, [128, D], F32)
    xres = Rot(S, 2, [128, D], F32)
    hTr = Rot(S, 2, [128, 8, 512], BF16)
    for i in range(2):
        hTr.items[i] = (hTr.items[i][0], [Buf() for _ in range(8)])
    aT = S.sb([128, 22, 512], BF16)
    BaT = [Buf() for _ in range(22)]
    sg = Rot(S, 2, [128, 512], F32)
    pg = Rot(S, 2, [128, 512], F32, psum=True)
    pu = Rot(S, 2, [128, 512], F32, psum=True)
    py = Rot(S, 2, [128, 512], F32, psum=True)

    def prep(g):
        hT, BhT = hTr.next()
        for s in range(4):
            xt, Bx = xin.next()
            r0 = g * 512 + s * 128
            S.op("sp", lambda e, xt=xt, r0=r0: e.dma_start(out=xt[:], in_=x_in[r0:r0 + 128, :]), writes=[Bx], dma=True)
            norm_transpose(S, C, xt, Bx, 128, hT, BhT, s * 128, nwT, BnwT)
        return hT, BhT

    def up(g, hT, BhT):
        for f in range(22):
            g_ps, Bg = pg.next()
            u_ps, Bu = pu.next()
            for k in range(8):
                S.op("pe", lambda e, k=k, f=f, g_ps=g_ps: e.matmul(g_ps[:, :], lhsT=w1[:, k, f * 128:(f + 1) * 128],
                                                                   rhs=hT[:, k, :], start=(k == 0), stop=(k == 7)),
                     reads=[Bw1[k], BhT[k]], writes=[Bg], acc=True)
            for k in range(8):
                S.op("pe", lambda e, k=k, f=f, u_ps=u_ps: e.matmul(u_ps[:, :], lhsT=w1[:, k, DFF + f * 128:DFF + (f + 1) * 128],
                                                                   rhs=hT[:, k, :], start=(k == 0), stop=(k == 7)),
                     reads=[Bw1[k], BhT[k]], writes=[Bu], acc=True)
            sgt, Bsg = sg.next()
            S.op("act", lambda e, sgt=sgt, g_ps=g_ps: e.activation(out=sgt[:], in_=g_ps[:], func=AF.Silu),
                 reads=[Bg], writes=[Bsg])
            S.op("dve", lambda e, sgt=sgt, u_ps=u_ps, f=f: e.tensor_tensor(out=aT[:, f, :], in0=sgt[:], in1=u_ps[:], op=ALU.mult),
                 reads=[Bsg, Bu], writes=[BaT[f]])

    def down(g):
        for s in range(4):
            xr, Bxr = xres.next()
            r0 = g * 512 + s * 128
            S.op("sp", lambda e, xr=xr, r0=r0: e.dma_start(out=xr[:], in_=x_in[r0:r0 + 128, :]), writes=[Bxr], dma=True)
            for dh in range(2):
                y_ps, By = py.next()
                for f in range(22):
                    S.op("pe", lambda e, f=f, s=s, dh=dh, y_ps=y_ps: e.matmul(
                        y_ps[:, :], lhsT=aT[:, f, s * 128:(s + 1) * 128], rhs=w2[:, f, dh * 512:(dh + 1) * 512],
                        start=(f == 0), stop=(f == 21)),
                        reads=[BaT[f], Bw2[f // 11]], writes=[By], acc=True)
                S.op("dve", lambda e, xr=xr, dh=dh, y_ps=y_ps: e.scalar_tensor_tensor(
                    out=xr[:, dh * 512:(dh + 1) * 512], in0=y_ps[:], scalar=0.5, in1=xr[:, dh * 512:(dh + 1) * 512],
                    op0=ALU.mult, op1=ALU.add), reads=[By, Bxr], writes=[Bxr])
            if fin_d is not None:
                st, Bs = rms_rstd(S, C, xr, Bxr, 128)
                S.op("dve", lambda e, xr=xr, st=st: e.scalar_tensor_tensor(
                    out=xr[:], in0=xr[:], scalar=st[:, 1:2], in1=finb[:], op0=ALU.mult, op1=ALU.mult),
                    reads=[Bxr, Bs, Bfin], writes=[Bxr])
            S.op("sp", lambda e, xr=xr, r0=r0: e.dma_start(out=x_out[r0:r0 + 128, :], in_=xr[:]), reads=[Bxr],
                 dma=True, is_out=True)

    import os
    stage = int(os.environ.get("FFN_STAGE", "3"))
    cur = prep(0)
    for g in range(NG):
        if stage >= 2:
            up(g, *cur)
        if g + 1 < NG:
            cur = prep(g + 1)
        if stage >= 3:
            down(g)


def colT(v):
    v = np.asarray(v, np.float32)
    return np.ascontiguousarray(v.reshape(-1, 128).T)


def build_ffn(T, final):
    nc = bass.Bass("TRN2", target_bir_lowering=False)
    x_in = nc.dram_tensor("x", [T, D], F32, kind="ExternalInput").ap()
    w_in = nc.dram_tensor("w_in", [D, 2 * DFF], F32, kind="ExternalInput").ap()
    w_out = nc.dram_tensor("w_out", [DFF, D], F32, kind="ExternalInput").ap()
    nwT = nc.dram_tensor("nwT", [128, 8], F32, kind="ExternalInput").ap()
    ident = nc.dram_tensor("ident", [128, 128], F32, kind="ExternalInput").ap()
    fin = nc.dram_tensor("fin", [128, D], F32, kind="ExternalInput").ap() if final else None
    x_out = nc.dram_tensor("y", [T, D], F32, kind="ExternalOutput").ap()
    S = Sched(nc)
    C = Common(S, ident)
    phase_ffn(S, C, T, x_in, x_out, w_in, w_out, nwT, fin)
    S.emit()
    return nc


def load_w_bf16(S, w_d, ktiles, ncols, chunks=None):
    w = S.sb([128, ktiles, ncols], BF16)
    Bw = [Buf() for _ in range(ktiles)]
    for k in range(ktiles):
        S.op("pool", lambda e, k=k: e.dma_start(out=w[:, k, :], in_=w_d[k * 128:(k + 1) * 128, :]),
             writes=[Bw[k]], dma=True)
    return w, Bw


def phase_acin(S, C, T, x_in, xh_in, w_in_d, nwT_d, QT, KT, V, UT):
    NG = T // 512
    w, Bw = load_w_bf16(S, w_in_d, 8, 5120)
    nwT = S.sb([128, 8], F32)
    BnwT = Buf()
    S.op("sp", lambda e: e.dma_start(out=nwT[:], in_=nwT_d), writes=[BnwT], dma=True)
    xin = Rot(S, 2, [128, D], F32)
    hTr = Rot(S, 2, [128, 8, 512], BF16)
    for i in range(2):
        hTr.items[i] = (hTr.items[i][0], [Buf() for _ in range(8)])
    pa = Rot(S, 3, [128, 512], F32, psum=True)
    pb = Rot(S, 3, [128, 512], F32, psum=True)
    ob = Rot(S, 4, [128, 512], BF16)
    of = Rot(S, 3, [128, 512], F32)
    sgr = Rot(S, 2, [128, 512], F32)

    def prep(g, ntok):
        hT, BhT = hTr.next()
        for s in range((ntok + 127) // 128):
            rows = min(128, ntok - s * 128)
            xt, Bx = xin.next()
            if g < NG:
                r0 = g * 512 + s * 128
                S.op("sp", lambda e, xt=xt, r0=r0: e.dma_start(out=xt[:], in_=x_in[r0:r0 + 128, :]), writes=[Bx], dma=True)
            else:
                S.op("sp", lambda e, xt=xt, rows=rows: e.dma_start(out=xt[0:rows, :], in_=xh_in[0:rows, :]), writes=[Bx], dma=True)
            norm_transpose(S, C, xt, Bx, rows, hT, BhT, s * 128, nwT, BnwT)
        return hT, BhT

    def fm(hT, BhT, col, ntok, ps, Bp):
        for k in range(8):
            S.op("pe", lambda e, k=k: e.matmul(ps[:, 0:ntok], lhsT=w[:, k, col:col + 128], rhs=hT[:, k, 0:ntok],
                                               start=(k == 0), stop=(k == 7)),
                 reads=[Bw[k], BhT[k]], writes=[Bp], acc=True)

    def group(g, hT, BhT, ntok):
        c0 = g * 512
        if g < NG:
            for h in range(8):
                ps, Bp = pa.next()
                fm(hT, BhT, h * 128, ntok, ps, Bp)
                o, Bo = ob.next()
                S.op("act", lambda e, ps=ps, o=o: e.activation(out=o[:], in_=ps[:], func=AF.Copy, scale=0.125),
                     reads=[Bp], writes=[Bo])
                S.op("sp", lambda e, o=o, h=h: e.dma_start(out=QT[h, :, c0:c0 + 512], in_=o[:]), reads=[Bo], dma=True, is_out=True)
                ps, Bp = pb.next()
                fm(hT, BhT, 1024 + h * 128, ntok, ps, Bp)
                o, Bo = ob.next()
                S.op("dve", lambda e, ps=ps, o=o: e.tensor_copy(out=o[:], in_=ps[:]), reads=[Bp], writes=[Bo])
                S.op("sp", lambda e, o=o, h=h: e.dma_start(out=KT[h, :, c0:c0 + 512], in_=o[:]), reads=[Bo], dma=True, is_out=True)
        for c in range(8):
            ps_g, Bpg = pa.next()
            fm(hT, BhT, 4096 + c * 128, ntok, ps_g, Bpg)
            ps_a, Bpa = pb.next()
            fm(hT, BhT, 3072 + c * 128, ntok, ps_a, Bpa)
            sg, Bsg = sgr.next()
            S.op("act", lambda e, ps_g=ps_g, sg=sg: e.activation(out=sg[:, 0:ntok], in_=ps_g[:, 0:ntok], func=AF.Sigmoid),
                 reads=[Bpg], writes=[Bsg])
            o, Bo = of.next()
            S.op("dve", lambda e, ps_a=ps_a, sg=sg, o=o: e.tensor_tensor(out=o[:, 0:ntok], in0=sg[:, 0:ntok], in1=ps_a[:, 0:ntok],
                                                                        op=ALU.mult), reads=[Bsg, Bpa], writes=[Bo])
            S.op("sp", lambda e, o=o, c=c: e.dma_start(out=UT[c, :, c0:c0 + ntok], in_=o[:, 0:ntok]), reads=[Bo], dma=True, is_out=True)
        if g < NG:
            for s in range(4):
                for half in range(2):
                    ps, Bp = (pa if half == 0 else pb).next()
                    for k in range(8):
                        S.op("pe", lambda e, k=k, ps=ps, s=s, half=half: e.matmul(
                            ps[:, :], lhsT=hT[:, k, s * 128:(s + 1) * 128], rhs=w[:, k, 2048 + half * 512:2048 + (half + 1) * 512],
                            start=(k == 0), stop=(k == 7)), reads=[Bw[k], BhT[k]], writes=[Bp], acc=True)
                    o, Bo = ob.next()
                    if half == 0:
                        S.op("act", lambda e, ps=ps, o=o: e.activation(out=o[:], in_=ps[:], func=AF.Copy), reads=[Bp], writes=[Bo])
                    else:
                        S.op("dve", lambda e, ps=ps, o=o: e.tensor_copy(out=o[:], in_=ps[:]), reads=[Bp], writes=[Bo])
                    r0 = c0 + s * 128
                    S.op("sp", lambda e, o=o, r0=r0, half=half: e.dma_start(out=V[r0:r0 + 128, half * 512:(half + 1) * 512], in_=o[:]),
                         reads=[Bo], dma=True, is_out=True)

    cur = prep(0, 512)
    for g in range(NG + 1):
        ntok = 512 if g < NG else 16
        nxt = None
        if g + 1 <= NG:
            nxt = prep(g + 1, 512 if g + 1 < NG else 16)
        group(g, cur[0], cur[1], ntok)
        cur = nxt


def att_kind(T, g, j):
    NKo = T // 128
    if j < NKo:
        dj = j - 4 * g
        if -1 <= dj <= 4:
            return ("sp", dj + 1)
        return ("c", 0 if dj < -1 else 1)
    jp = j - NKo
    if g == T // 512 - 1 and jp == NKo - 1:
        return ("sp", 6)
    return ("c", 2)


def phase_att(S, C, T, QT, KTf, Vf, BT_d, cb_d, lamb_d, sw_d, lam_init, OAT):
    NG = T // 512
    NK = 2 * T // 128
    cb = S.sb([128, 24], F32)
    Bcb = Buf()
    S.op("sp", lambda e: e.dma_start(out=cb[:], in_=cb_d), writes=[Bcb], dma=True)
    lamb = S.sb([128, 256], F32)
    Blamb = Buf()
    S.op("sp", lambda e: e.dma_start(out=lamb[:], in_=lamb_d), writes=[Blamb], dma=True)
    sc = S.sb([128, 8], F32)
    Bsc = Buf()
    S.op("sp", lambda e: e.dma_start(out=sc[:, 7:8], in_=sw_d), writes=[Bsc], dma=True)
    tmp = S.sb([128, 128], F32)
    Btmp = Buf()
    S.op("dve", lambda e: e.tensor_tensor(out=tmp[:, 0:64], in0=lamb[:, 0:64], in1=lamb[:, 64:128], op=ALU.mult),
         reads=[Blamb], writes=[Btmp])
    S.op("dve", lambda e: e.tensor_tensor(out=tmp[:, 64:128], in0=lamb[:, 128:192], in1=lamb[:, 192:256], op=ALU.mult),
         reads=[Blamb], writes=[Btmp])
    S.op("dve", lambda e: e.reduce_sum(out=sc[:, 0:1], in_=tmp[:, 0:64], axis=AX.X), reads=[Btmp], writes=[Bsc])
    S.op("dve", lambda e: e.reduce_sum(out=sc[:, 1:2], in_=tmp[:, 64:128], axis=AX.X), reads=[Btmp], writes=[Bsc])
    S.op("act", lambda e: e.activation(out=sc[:, 2:4], in_=sc[:, 0:2], func=AF.Exp), reads=[Bsc], writes=[Bsc])
    S.op("dve", lambda e: e.tensor_tensor(out=sc[:, 4:5], in0=sc[:, 3:4], in1=sc[:, 2:3], op=ALU.subtract), reads=[Bsc], writes=[Bsc])
    S.op("dve", lambda e: e.tensor_scalar_add(out=sc[:, 4:5], in0=sc[:, 4:5], scalar1=-float(lam_init)), reads=[Bsc], writes=[Bsc])
    S.op("dve", lambda e: e.tensor_scalar_mul(out=sc[:, 5:6], in0=sc[:, 7:8], scalar1=1.0 - float(lam_init)), reads=[Bsc], writes=[Bsc])
    ones = S.sb([128, 128], F32)
    Bones = Buf()
    S.op("pool", lambda e: e.memset(ones[:], 1.0), writes=[Bones])

    KTr = Rot(S, 2, [128, 2 * T], BF16)
    Vr = Rot(S, 2, [128, NK, 128], BF16)
    QTr = Rot(S, 2, [128, T], BF16)
    Btr = Rot(S, 2, [128, 7, 512], F32)
    ps1 = Rot(S, 2, [128, 512], F32, psum=True)
    ps2 = Rot(S, 2, [128, 512], F32, psum=True)
    po1 = S.ps([128, 512], F32)
    po2 = S.ps([128, 512], F32)
    Bpo1, Bpo2 = Buf(), Buf()
    pe1 = S.ps([128, 512], F32)
    pe2 = S.ps([128, 512], F32)
    Bpe1, Bpe2 = Buf(), Buf()
    p1r = Rot(S, 3, [128, 512], BF16)
    p2r = Rot(S, 3, [128, 512], BF16)
    sbr = Rot(S, 4, [128, 512], F32)
    acc1r = Rot(S, 2, [128, 512], F32)
    acc2r = Rot(S, 2, [128, 512], F32)
    ep = Rot(S, 6, [128, 512], F32)
    outr = Rot(S, 2, [128, 512], BF16)

    units = [(h, g, j) for h in range(8) for g in range(NG) for j in range(NK)]
    heads = {}

    def load_head(h):
        KTh, BK = KTr.next()
        Vh, BV = Vr.next()
        QTh, BQ = QTr.next()
        Bth, BB = Btr.next()
        S.op("sp", lambda e: e.dma_start(out=KTh[:], in_=KTf[h, :, :]), writes=[BK], dma=True)
        S.op("sp", lambda e: e.dma_start(out=Vh[:], in_=Vf[:, h * 128:(h + 1) * 128].rearrange("(kt p) e -> p kt e", p=128)),
             writes=[BV], dma=True)
        S.op("sp", lambda e: e.dma_start(out=QTh[:], in_=QT[h, :, :]), writes=[BQ], dma=True)
        S.op("sp", lambda e: e.dma_start(out=Bth[:], in_=BT_d[h].rearrange("i p q -> p i q")), writes=[BB], dma=True)
        heads[h] = (KTh, BK, Vh, BV, QTh, BQ, Bth, BB)

    state = {}

    def s_mm(u):
        h, g, j = u
        if h not in heads:
            load_head(h)
        KTh, BK, Vh, BV, QTh, BQ, Bth, BB = heads[h]
        s1, Bs1 = ps1.next()
        s2, Bs2 = ps2.next()
        S.op("pe", lambda e: e.matmul(s1[:, :], lhsT=KTh[0:64, j * 128:(j + 1) * 128], rhs=QTh[0:64, g * 512:(g + 1) * 512],
                                      start=True, stop=True), reads=[BK, BQ], writes=[Bs1])
        S.op("pe", lambda e: e.matmul(s2[:, :], lhsT=KTh[64:128, j * 128:(j + 1) * 128], rhs=QTh[64:128, g * 512:(g + 1) * 512],
                                      start=True, stop=True), reads=[BK, BQ], writes=[Bs2])
        state[u] = (s1, Bs1, s2, Bs2)

    def soft(u):
        h, g, j = u
        KTh, BK, Vh, BV, QTh, BQ, Bth, BB = heads[h]
        s1, Bs1, s2, Bs2 = state[u]
        p1, Bp1 = p1r.next()
        p2, Bp2 = p2r.next()
        kind, idx = att_kind(T, g, j)
        if kind == "sp":
            t1, Bt1 = sbr.next()
            t2, Bt2 = sbr.next()
            S.op("dve", lambda e: e.tensor_tensor(out=t1[:], in0=s1[:], in1=Bth[:, idx, :], op=ALU.add), reads=[Bs1, BB], writes=[Bt1])
            S.op("dve", lambda e: e.tensor_tensor(out=t2[:], in0=s2[:], in1=Bth[:, idx, :], op=ALU.add), reads=[Bs2, BB], writes=[Bt2])
            S.op("act", lambda e: e.activation(out=p1[:], in_=t1[:], func=AF.Exp), reads=[Bt1], writes=[Bp1])
            S.op("act", lambda e: e.activation(out=p2[:], in_=t2[:], func=AF.Exp), reads=[Bt2], writes=[Bp2])
        else:
            col = idx * 8 + h
            S.op("act", lambda e: e.activation(out=p1[:], in_=s1[:], func=AF.Exp, bias=cb[:, col:col + 1]), reads=[Bs1, Bcb], writes=[Bp1])
            S.op("act", lambda e: e.activation(out=p2[:], in_=s2[:], func=AF.Exp, bias=cb[:, col:col + 1]), reads=[Bs2, Bcb], writes=[Bp2])
        if j == 0:
            state[(h, g)] = (acc1r.next(), acc2r.next())
            (a1, Ba1), (a2, Ba2) = state[(h, g)]
            S.op("dve", lambda e: e.tensor_copy(out=a1[:], in_=p1[:]), reads=[Bp1], writes=[Ba1])
            S.op("dve", lambda e: e.tensor_copy(out=a2[:], in_=p2[:]), reads=[Bp2], writes=[Ba2])
        else:
            (a1, Ba1), (a2, Ba2) = state[(h, g)]
            S.op("dve", lambda e: e.tensor_tensor(out=a1[:], in0=a1[:], in1=p1[:], op=ALU.add), reads=[Bp1, Ba1], writes=[Ba1])
            S.op("dve", lambda e: e.tensor_tensor(out=a2[:], in0=a2[:], in1=p2[:], op=ALU.add), reads=[Bp2, Ba2], writes=[Ba2])
        state[u] = (p1, Bp1, p2, Bp2)

    def pv(u):
        h, g, j = u
        KTh, BK, Vh, BV, QTh, BQ, Bth, BB = heads[h]
        p1, Bp1, p2, Bp2 = state.pop(u)
        S.op("pe", lambda e: e.matmul(po1[:, :], lhsT=Vh[:, j, :], rhs=p1[:], start=(j == 0), stop=(j == NK - 1)),
             reads=[BV, Bp1], writes=[Bpo1], acc=True)
        S.op("pe", lambda e: e.matmul(po2[:, :], lhsT=Vh[:, j, :], rhs=p2[:], start=(j == 0), stop=(j == NK - 1)),
             reads=[BV, Bp2], writes=[Bpo2], acc=True)
        if j == NK - 1:
            epilogue(h, g)

    def epilogue(h, g):
        (a1, Ba1), (a2, Ba2) = state.pop((h, g))
        S.op("pe", lambda e: e.matmul(pe1[:, :], lhsT=ones[:], rhs=a1[:], start=True, stop=True), reads=[Bones, Ba1], writes=[Bpe1])
        S.op("pe", lambda e: e.matmul(pe2[:, :], lhsT=ones[:], rhs=a2[:], start=True, stop=True), reads=[Bones, Ba2], writes=[Bpe2])
        r1, Br1 = ep.next()
        r2, Br2 = ep.next()
        S.op("dve", lambda e: e.reciprocal(out=r1[:], in_=pe1[:]), reads=[Bpe1], writes=[Br1])
        S.op("dve", lambda e: e.tensor_tensor(out=r1[:], in0=po1[:], in1=r1[:], op=ALU.mult), reads=[Bpo1, Br1], writes=[Br1])
        S.op("dve", lambda e: e.reciprocal(out=r2[:], in_=pe2[:]), reads=[Bpe2], writes=[Br2])
        S.op("dve", lambda e: e.tensor_tensor(out=r2[:], in0=po2[:], in1=r2[:], op=ALU.mult), reads=[Bpo2, Br2], writes=[Br2])
        o, Bo = ep.next()
        S.op("dve", lambda e: e.scalar_tensor_tensor(out=o[:], in0=r2[:], scalar=sc[:, 4:5], in1=r1[:], op0=ALU.mult, op1=ALU.add),
             reads=[Br1, Br2, Bsc], writes=[Bo])
        sq, Bsq = ep.next()
        S.op("act", lambda e: e.activation(out=sq[:], in_=o[:], func=AF.Square), reads=[Bo], writes=[Bsq])
        S.op("pe", lambda e: e.matmul(pe1[:, :], lhsT=ones[:], rhs=sq[:], start=True, stop=True), reads=[Bones, Bsq], writes=[Bpe1])
        S.op("act", lambda e: e.activation(out=sq[:], in_=pe1[:], func=AF.Sqrt, bias=EPS, scale=1.0 / 128), reads=[Bpe1], writes=[Bsq])
        S.op("dve", lambda e: e.reciprocal(out=sq[:], in_=sq[:]), reads=[Bsq], writes=[Bsq])
        ot, Bot = outr.next()
        S.op("dve", lambda e: e.scalar_tensor_tensor(out=ot[:], in0=o[:], scalar=sc[:, 5:6], in1=sq[:], op0=ALU.mult, op1=ALU.mult),
             reads=[Bo, Bsq, Bsc], writes=[Bot])
        S.op("sp", lambda e: e.dma_start(out=OAT[h, :, g * 512:(g + 1) * 512], in_=ot[:]), reads=[Bot], dma=True, is_out=True)

    s_mm(units[0])
    for i, u in enumerate(units):
        soft(u)
        if i + 1 < len(units):
            s_mm(units[i + 1])
        pv(u)


def phase_convout(S, C, T, UT, cw_d, cv_d, OAT, w_out_d, x_in, x_out):
    NB = T // 512
    wo, Bwo = load_w_bf16(S, w_out_d, 16, D)
    cw = S.sb([128, 8, 31], F32)
    Bcw = Buf()
    S.op("sp", lambda e: e.dma_start(out=cw[:], in_=cw_d), writes=[Bcw], dma=True)
    cv = S.sb([128, 3, 8], F32)
    Bcv = Buf()
    S.op("sp", lambda e: e.dma_start(out=cv[:], in_=cv_d), writes=[Bcv], dma=True)
    ones = S.sb([128, 128], F32)
    Bones = Buf()
    S.op("pool", lambda e: e.memset(ones[:], 1.0), writes=[Bones])
    ur = Rot(S, 3, [128, 544], F32)
    vr = Rot(S, 2, [128, 8, 512], F32)
    sqr = Rot(S, 3, [128, 512], F32)
    str_ = Rot(S, 3, [128, 512], F32)
    obr = Rot(S, 2, [128, 8, 512], BF16)
    oar = Rot(S, 2, [128, 8, 512], BF16)
    xr_ = Rot(S, 3, [128, D], F32)
    pst = Rot(S, 2, [128, 512], F32, psum=True)
    py = Rot(S, 2, [128, 512], F32, psum=True)

    def block(tb):
        t0 = tb * 512
        v, Bv = vr.next()
        for c in range(8):
            u, Bu = ur.next()
            if tb == 0:
                S.op("pool", lambda e, u=u: e.memset(u[:, 0:15], 0.0), writes=[Bu])
                S.op("sp", lambda e, u=u, c=c: e.dma_start(out=u[:, 15:542], in_=UT[c, :, 0:527]), writes=[Bu], dma=True)
            else:
                S.op("sp", lambda e, u=u, c=c: e.dma_start(out=u[:, 0:542], in_=UT[c, :, t0 - 15:t0 + 527]), writes=[Bu], dma=True)
            S.op("dve", lambda e, u=u, c=c: e.tensor_scalar(out=v[:, c, :], in0=u[:, 0:512], scalar1=cw[:, c, 0:1], scalar2=cv[:, 0, c:c + 1],
                                                            op0=ALU.mult, op1=ALU.add), reads=[Bu, Bcw, Bcv], writes=[Bv])
            for j in range(1, 31):
                S.op("dve", lambda e, u=u, c=c, j=j: e.scalar_tensor_tensor(out=v[:, c, :], in0=u[:, j:j + 512], scalar=cw[:, c, j:j + 1],
                                                                            in1=v[:, c, :], op0=ALU.mult, op1=ALU.add),
                     reads=[Bu, Bcw, Bv], writes=[Bv])
        pm, Bpm = pst.next()
        for c in range(8):
            S.op("pe", lambda e, c=c: e.matmul(pm[:, :], lhsT=ones[:], rhs=v[:, c, :], start=(c == 0), stop=(c == 7)),
                 reads=[Bones, Bv], writes=[Bpm], acc=True)
        mS, BmS = str_.next()
        S.op("act", lambda e: e.activation(out=mS[:], in_=pm[:], func=AF.Identity, scale=1.0 / 1024), reads=[Bpm], writes=[BmS])
        pv_, Bpv = pst.next()
        for c in range(8):
            S.op("dve", lambda e, c=c: e.tensor_tensor(out=v[:, c, :], in0=v[:, c, :], in1=mS[:], op=ALU.subtract), reads=[Bv, BmS], writes=[Bv])
            sq, Bsq = sqr.next()
            S.op("act", lambda e, c=c, sq=sq: e.activation(out=sq[:], in_=v[:, c, :], func=AF.Square), reads=[Bv], writes=[Bsq])
            S.op("pe", lambda e, c=c, sq=sq: e.matmul(pv_[:, :], lhsT=ones[:], rhs=sq[:], start=(c == 0), stop=(c == 7)),
                 reads=[Bones, Bsq], writes=[Bpv], acc=True)
        rs, Brs = str_.next()
        S.op("act", lambda e: e.activation(out=rs[:], in_=pv_[:], func=AF.Sqrt, bias=EPS, scale=1.0 / 1024), reads=[Bpv], writes=[Brs])
        S.op("dve", lambda e: e.reciprocal(out=rs[:], in_=rs[:]), reads=[Brs], writes=[Brs])
        ob, Bob = obr.next()
        for c in range(8):
            S.op("dve", lambda e, c=c: e.tensor_tensor(out=v[:, c, :], in0=v[:, c, :], in1=rs[:], op=ALU.mult), reads=[Bv, Brs], writes=[Bv])
            S.op("act", lambda e, c=c: e.activation(out=ob[:, c, :], in_=v[:, c, :], func=AF.Silu, scale=cv[:, 1, c:c + 1], bias=cv[:, 2, c:c + 1]),
                 reads=[Bv, Bcv], writes=[Bob])
        oa, Boa = oar.next()
        S.op("sp", lambda e: e.dma_start(out=oa[:], in_=OAT[:, :, t0:t0 + 512].rearrange("h p t -> p h t")), writes=[Boa], dma=True)
        for s in range(4):
            xr, Bxr = xr_.next()
            r0 = t0 + s * 128
            S.op("sp", lambda e, xr=xr, r0=r0: e.dma_start(out=xr[:], in_=x_in[r0:r0 + 128, :]), writes=[Bxr], dma=True)
            for dh in range(2):
                y_ps, By = py.next()
                for ct in range(16):
                    src, Bsrc = (oa, Boa) if ct < 8 else (ob, Bob)
                    S.op("pe", lambda e, ct=ct, src=src, s=s, dh=dh, y_ps=y_ps: e.matmul(
                        y_ps[:, :], lhsT=src[:, ct % 8, s * 128:(s + 1) * 128], rhs=wo[:, ct, dh * 512:(dh + 1) * 512],
                        start=(ct == 0), stop=(ct == 15)), reads=[Bsrc, Bwo[ct]], writes=[By], acc=True)
                S.op("dve", lambda e, xr=xr, dh=dh, y_ps=y_ps: e.tensor_tensor(out=xr[:, dh * 512:(dh + 1) * 512], in0=y_ps[:],
                                                                               in1=xr[:, dh * 512:(dh + 1) * 512], op=ALU.add),
                     reads=[By, Bxr], writes=[Bxr])
            S.op("sp", lambda e, xr=xr, r0=r0: e.dma_start(out=x_out[r0:r0 + 128, :], in_=xr[:]), reads=[Bxr], dma=True, is_out=True)

    for tb in range(NB):
        block(tb)


def rel_bucket_np(rel):
    rel = np.asarray(rel, np.int64)
    half, max_exact = 16, 8
    ret = np.where(rel > 0, half, 0)
    n = np.abs(rel)
    nf = np.maximum(n, 1).astype(np.float32)
    large = max_exact + (np.log(nf / np.float32(max_exact)) / np.float32(np.log(128 / max_exact))
                         * np.float32(half - max_exact)).astype(np.int32)
    large = np.minimum(large, half - 1)
    return ret + np.where(n < max_exact, n, large)


def make_bias_tiles(rel_bias, parity):
    sgn = 1 if parity == 0 else -1
    kk = np.arange(128)[:, None]
    qq = np.arange(512)[None, :]
    out = np.empty((8, 7, 128, 512), np.float32)
    for i in range(6):
        dj = i - 1
        b = rel_bucket_np(sgn * (128 * dj + kk - qq))
        out[:, i] = np.moveaxis(rel_bias[b], -1, 0)
    b = rel_bucket_np(sgn * (639 - kk - qq))
    out[:, 6] = np.moveaxis(rel_bias[b], -1, 0)
    return out


def make_cb(rel_bias, parity):
    lo, hi = (15, 31) if parity == 0 else (31, 15)
    par = 31 if parity == 0 else 15
    row = np.concatenate([rel_bias[lo], rel_bias[hi], rel_bias[par]]).astype(np.float32)
    return np.ascontiguousarray(np.broadcast_to(row, (128, 24)))


def build_acin(T):
    nc = bass.Bass("TRN2", target_bir_lowering=False)
    x_in = nc.dram_tensor("x", [T, D], F32, kind="ExternalInput").ap()
    xh = nc.dram_tensor("xh", [16, D], F32, kind="ExternalInput").ap()
    w_in = nc.dram_tensor("w_in", [D, 5120], F32, kind="ExternalInput").ap()
    nwT = nc.dram_tensor("nwT", [128, 8], F32, kind="ExternalInput").ap()
    ident = nc.dram_tensor("ident", [128, 128], F32, kind="ExternalInput").ap()
    QT = nc.dram_tensor("QT", [8, 128, T], BF16, kind="ExternalOutput").ap()
    KT = nc.dram_tensor("KT", [8, 128, T], BF16, kind="ExternalOutput").ap()
    V = nc.dram_tensor("V", [T, D], BF16, kind="ExternalOutput").ap()
    UT = nc.dram_tensor("UT", [8, 128, T + 16], F32, kind="ExternalOutput").ap()
    S = Sched(nc)
    C = Common(S, ident)
    phase_acin(S, C, T, x_in, xh, w_in, nwT, QT, KT, V, UT)
    S.emit()
    return nc


def build_att(T, lam_init):
    nc = bass.Bass("TRN2", target_bir_lowering=False)
    QT = nc.dram_tensor("QT", [8, 128, T], BF16, kind="ExternalInput").ap()
    KTf = nc.dram_tensor("KTf", [8, 128, 2 * T], BF16, kind="ExternalInput").ap()
    Vf = nc.dram_tensor("Vf", [2 * T, D], BF16, kind="ExternalInput").ap()
    BT = nc.dram_tensor("BT", [8, 7, 128, 512], F32, kind="ExternalInput").ap()
    cb = nc.dram_tensor("cb", [128, 24], F32, kind="ExternalInput").ap()
    lamb = nc.dram_tensor("lamb", [128, 256], F32, kind="ExternalInput").ap()
    sw = nc.dram_tensor("sw", [128, 1], F32, kind="ExternalInput").ap()
    ident = nc.dram_tensor("ident", [128, 128], F32, kind="ExternalInput").ap()
    OAT = nc.dram_tensor("OAT", [8, 128, T], BF16, kind="ExternalOutput").ap()
    S = Sched(nc)
    C = Common(S, ident)
    phase_att(S, C, T, QT, KTf, Vf, BT, cb, lamb, sw, lam_init, OAT)
    S.emit()
    return nc


def build_convout(T):
    nc = bass.Bass("TRN2", target_bir_lowering=False)
    UT = nc.dram_tensor("UT", [8, 128, T + 16], F32, kind="ExternalInput").ap()
    cw = nc.dram_tensor("cw", [128, 8, 31], F32, kind="ExternalInput").ap()
    cv = nc.dram_tensor("cv", [128, 3, 8], F32, kind="ExternalInput").ap()
    OAT = nc.dram_tensor("OAT", [8, 128, T], BF16, kind="ExternalInput").ap()
    w_out = nc.dram_tensor("w_out", [2048, D], F32, kind="ExternalInput").ap()
    x_in = nc.dram_tensor("x", [T, D], F32, kind="ExternalInput").ap()
    ident = nc.dram_tensor("ident", [128, 128], F32, kind="ExternalInput").ap()
    x_out = nc.dram_tensor("y", [T, D], F32, kind="ExternalOutput").ap()
    S = Sched(nc)
    C = Common(S, ident)
    phase_convout(S, C, T, UT, cw, cv, OAT, w_out, x_in, x_out)
    S.emit()
    return nc


def phase_gdin(S, C, T, x_in, xh_in, w_in_d, nwT_d, ab_d, PT, SZ, BG):
    NG = T // 512
    w, Bw = load_w_bf16(S, w_in_d, 8, 4128)
    nwT = S.sb([128, 8], F32)
    BnwT = Buf()
    S.op("sp", lambda e: e.dma_start(out=nwT[:], in_=nwT_d), writes=[BnwT], dma=True)
    ab = S.sb([128, 32], F32)
    Bab = Buf()
    S.op("sp", lambda e: e.dma_start(out=ab[:], in_=ab_d), writes=[Bab], dma=True)
    S.op("act", lambda e: e.activation(out=ab[:, 0:16], in_=ab[:, 0:16], func=AF.Exp), reads=[Bab], writes=[Bab])
    xin = Rot(S, 2, [128, D], F32)
    hTr = Rot(S, 2, [128, 8, 512], BF16)
    for i in range(2):
        hTr.items[i] = (hTr.items[i][0], [Buf() for _ in range(8)])
    pa = Rot(S, 3, [128, 512], F32, psum=True)
    pb = Rot(S, 3, [128, 512], F32, psum=True)
    of = Rot(S, 4, [128, 512], F32)
    bgr = Rot(S, 2, [128, 96], F32)

    def prep(g, ntok):
        hT, BhT = hTr.next()
        for s in range((ntok + 127) // 128):
            rows = min(128, ntok - s * 128)
            xt, Bx = xin.next()
            if g < NG:
                r0 = g * 512 + s * 128
                S.op("sp", lambda e, xt=xt, r0=r0: e.dma_start(out=xt[:], in_=x_in[r0:r0 + 128, :]), writes=[Bx], dma=True)
            else:
                S.op("sp", lambda e, xt=xt, rows=rows: e.dma_start(out=xt[0:rows, :], in_=xh_in[0:rows, :]), writes=[Bx], dma=True)
            norm_transpose(S, C, xt, Bx, rows, hT, BhT, s * 128, nwT, BnwT)
        return hT, BhT

    def group(g, hT, BhT, ntok):
        c0 = g * 512
        for ft in range(24):
            use_act = (ft % 2 == 0)
            ps, Bp = (pa if use_act else pb).next()
            for k in range(8):
                S.op("pe", lambda e, k=k, ps=ps, ft=ft: e.matmul(ps[:, 0:ntok], lhsT=w[:, k, ft * 128:(ft + 1) * 128], rhs=hT[:, k, 0:ntok],
                                                                 start=(k == 0), stop=(k == 7)), reads=[Bw[k], BhT[k]], writes=[Bp], acc=True)
            o, Bo = of.next()
            if use_act:
                S.op("act", lambda e, ps=ps, o=o: e.activation(out=o[:, 0:ntok], in_=ps[:, 0:ntok], func=AF.Copy), reads=[Bp], writes=[Bo])
            else:
                S.op("dve", lambda e, ps=ps, o=o: e.tensor_copy(out=o[:, 0:ntok], in_=ps[:, 0:ntok]), reads=[Bp], writes=[Bo])
            S.op("sp", lambda e, o=o, ft=ft: e.dma_start(out=PT[ft, :, c0:c0 + ntok], in_=o[:, 0:ntok]), reads=[Bo], dma=True, is_out=True)
        if g >= NG:
            return
        for s in range(4):
            r0 = c0 + s * 128
            for half in range(2):
                ps, Bp = pa.next()
                for k in range(8):
                    S.op("pe", lambda e, k=k, ps=ps, half=half: e.matmul(
                        ps[:, :], lhsT=hT[:, k, s * 128:(s + 1) * 128], rhs=w[:, k, 3072 + half * 512:3072 + (half + 1) * 512],
                        start=(k == 0), stop=(k == 7)), reads=[Bw[k], BhT[k]], writes=[Bp], acc=True)
                o, Bo = of.next()
                S.op("act", lambda e, ps=ps, o=o: e.activation(out=o[:], in_=ps[:], func=AF.Silu), reads=[Bp], writes=[Bo])
                S.op("sp", lambda e, o=o, half=half: e.dma_start(out=SZ[r0:r0 + 128, half * 512:(half + 1) * 512], in_=o[:]),
                     reads=[Bo], dma=True, is_out=True)
            ps, Bp = pb.next()
            for k in range(8):
                S.op("pe", lambda e, k=k, ps=ps: e.matmul(ps[:, 0:32], lhsT=hT[:, k, s * 128:(s + 1) * 128], rhs=w[:, k, 4096:4128],
                                                          start=(k == 0), stop=(k == 7)), reads=[Bw[k], BhT[k]], writes=[Bp], acc=True)
            bg, Bbg = bgr.next()
            S.op("dve", lambda e, ps=ps, bg=bg: e.tensor_copy(out=bg[:, 32:64], in_=ps[:, 0:32]), reads=[Bp], writes=[Bbg])
            S.op("act", lambda e, bg=bg: e.activation(out=bg[:, 64:80], in_=bg[:, 32:48], func=AF.Exp, scale=-1.0), reads=[Bbg], writes=[Bbg])
            S.op("dve", lambda e, bg=bg: e.tensor_scalar_add(out=bg[:, 64:80], in0=bg[:, 64:80], scalar1=1.0), reads=[Bbg], writes=[Bbg])
            S.op("dve", lambda e, bg=bg: e.reciprocal(out=bg[:, 0:16], in_=bg[:, 64:80]), reads=[Bbg], writes=[Bbg])
            S.op("dve", lambda e, bg=bg: e.tensor_tensor(out=bg[:, 80:96], in0=bg[:, 48:64], in1=ab[:, 16:32], op=ALU.add), reads=[Bbg, Bab], writes=[Bbg])
            S.op("act", lambda e, bg=bg: e.activation(out=bg[:, 80:96], in_=bg[:, 80:96], func=AF.Exp), reads=[Bbg], writes=[Bbg])
            S.op("act", lambda e, bg=bg: e.activation(out=bg[:, 80:96], in_=bg[:, 80:96], func=AF.Ln, bias=1.0), reads=[Bbg], writes=[Bbg])
            S.op("dve", lambda e, bg=bg: e.scalar_tensor_tensor(out=bg[:, 16:32], in0=bg[:, 80:96], scalar=-1.0, in1=ab[:, 0:16],
                                                               op0=ALU.mult, op1=ALU.mult), reads=[Bbg, Bab], writes=[Bbg])
            S.op("sp", lambda e, bg=bg: e.dma_start(out=BG[r0:r0 + 128, :], in_=bg[:, 0:32]), reads=[Bbg], dma=True, is_out=True)

    cur = prep(0, 512)
    for g in range(NG + 1):
        ntok = 512 if g < NG else 16
        nxt = None
        if g + 1 <= NG:
            nxt = prep(g + 1, 512 if g + 1 < NG else 16)
        group(g, cur[0], cur[1], ntok)
        cur = nxt


def phase_gdprep(S, C, T, PT, cw_d, QnT, KnT, Kn, Vt):
    NB = T // 512
    cw = S.sb([128, 24, 5], F32)
    Bcw = Buf()
    S.op("sp", lambda e: e.dma_start(out=cw[:], in_=cw_d), writes=[Bcw], dma=True)
    ones = S.sb([128, 128], F32)
    Bones = Buf()
    S.op("pool", lambda e: e.memset(ones[:], 1.0), writes=[Bones])
    ur = Rot(S, 3, [128, 516], F32)
    cr = Rot(S, 3, [128, 512], F32)
    sqr = Rot(S, 2, [128, 512], F32)
    rnr = Rot(S, 2, [128, 512], F32)
    nbr = Rot(S, 3, [128, 512], BF16)
    ktm = Rot(S, 2, [128, 4, D], BF16)
    vtm = Rot(S, 2, [128, 4, D], BF16)
    pss = Rot(S, 2, [128, 512], F32, psum=True)
    ptA = Rot(S, 2, [128, 4, 128], BF16, psum=True)
    ptD = Rot(S, 2, [128, 4, 128], BF16, psum=True)

    def block(tb):
        t0 = tb * 512
        kt, Bkt = ktm.next()
        vt, Bvt = vtm.next()
        for ft in range(24):
            u, Bu = ur.next()
            if tb == 0:
                S.op("pool", lambda e, u=u: e.memset(u[:, 0:2], 0.0), writes=[Bu])
                S.op("sp", lambda e, u=u, ft=ft: e.dma_start(out=u[:, 2:516], in_=PT[ft, :, 0:514]), writes=[Bu], dma=True)
            else:
                S.op("sp", lambda e, u=u, ft=ft: e.dma_start(out=u[:, 0:516], in_=PT[ft, :, t0 - 2:t0 + 514]), writes=[Bu], dma=True)
            c, Bc = cr.next()
            S.op("dve", lambda e, u=u, c=c, ft=ft: e.tensor_scalar_mul(out=c[:], in0=u[:, 0:512], scalar1=cw[:, ft, 0:1]),
                 reads=[Bu, Bcw], writes=[Bc])
            for j in range(1, 5):
                S.op("dve", lambda e, u=u, c=c, ft=ft, j=j: e.scalar_tensor_tensor(out=c[:], in0=u[:, j:j + 512], scalar=cw[:, ft, j:j + 1], in1=c[:],
                                                                                   op0=ALU.mult, op1=ALU.add), reads=[Bu, Bcw, Bc], writes=[Bc])
            S.op("act", lambda e, c=c: e.activation(out=c[:], in_=c[:], func=AF.Silu), reads=[Bc], writes=[Bc])
            nb, Bnb = nbr.next()
            h = ft % 8
            if ft < 16:
                sq, Bsq = sqr.next()
                S.op("act", lambda e, c=c, sq=sq: e.activation(out=sq[:], in_=c[:], func=AF.Square), reads=[Bc], writes=[Bsq])
                ps, Bps = pss.next()
                S.op("pe", lambda e, ps=ps, sq=sq: e.matmul(ps[:, :], lhsT=ones[:], rhs=sq[:], start=True, stop=True), reads=[Bones, Bsq], writes=[Bps])
                rn, Brn = rnr.next()
                S.op("act", lambda e, ps=ps, rn=rn: e.activation(out=rn[:], in_=ps[:], func=AF.Sqrt, bias=EPS, scale=1.0), reads=[Bps], writes=[Brn])
                S.op("dve", lambda e, rn=rn: e.reciprocal(out=rn[:], in_=rn[:]), reads=[Brn], writes=[Brn])
                if ft < 8:
                    S.op("dve", lambda e, c=c, rn=rn, nb=nb: e.scalar_tensor_tensor(out=nb[:], in0=c[:], scalar=float(128 ** -0.5), in1=rn[:],
                                                                                   op0=ALU.mult, op1=ALU.mult), reads=[Bc, Brn], writes=[Bnb])
                    S.op("sp", lambda e, nb=nb, h=h: e.dma_start(out=QnT[h, :, t0:t0 + 512], in_=nb[:]), reads=[Bnb], dma=True, is_out=True)
                else:
                    S.op("dve", lambda e, c=c, rn=rn, nb=nb: e.tensor_tensor(out=nb[:], in0=c[:], in1=rn[:], op=ALU.mult), reads=[Bc, Brn], writes=[Bnb])
                    S.op("sp", lambda e, nb=nb, h=h: e.dma_start(out=KnT[h, :, t0:t0 + 512], in_=nb[:]), reads=[Bnb], dma=True, is_out=True)
            else:
                S.op("act", lambda e, c=c, nb=nb: e.activation(out=nb[:], in_=c[:], func=AF.Copy), reads=[Bc], writes=[Bnb])
            if ft >= 8:
                use_act = (ft % 2 == 0)
                pt, Bpt = (ptA if use_act else ptD).next()
                for s in range(4):
                    S.op("pe", lambda e, s=s, pt=pt, nb=nb: e.transpose(out=pt[:, s, :], in_=nb[:, s * 128:(s + 1) * 128], identity=C.idb[:]),
                         reads=[Bnb, C.Bidb], writes=[Bpt], acc=True)
                dst, Bdst = (kt, Bkt) if ft < 16 else (vt, Bvt)
                if use_act:
                    S.op("act", lambda e, pt=pt, dst=dst, h=h: e.activation(out=dst[:, :, h * 128:(h + 1) * 128], in_=pt[:, :, :], func=AF.Copy),
                         reads=[Bpt], writes=[Bdst])
                else:
                    S.op("dve", lambda e, pt=pt, dst=dst, h=h: e.tensor_copy(out=dst[:, :, h * 128:(h + 1) * 128], in_=pt[:, :, :]),
                         reads=[Bpt], writes=[Bdst])
        S.op("sp", lambda e: e.dma_start(out=Kn[t0:t0 + 512, :].rearrange("(s p) d -> p s d", p=128), in_=kt[:]), reads=[Bkt], dma=True, is_out=True)
        S.op("sp", lambda e: e.dma_start(out=Vt[t0:t0 + 512, :].rearrange("(s p) d -> p s d", p=128), in_=vt[:]), reads=[Bvt], dma=True, is_out=True)

    for tb in range(NB):
        block(tb)


def run_rr(gens):
    active = list(gens)
    while active:
        for g in list(active):
            try:
                next(g)
            except StopIteration:
                active.remove(g)


def gd_masks(reverse):
    i = np.arange(128)[:, None]
    c = np.arange(128)[None, :]
    same = (i // 64) == (c // 64)
    if not reverse:
        MU = same & (i <= c)
        MGT = same & (i > c)
        MI = same & (i >= c)
    else:
        MU = same & (i >= c)
        MGT = same & (i < c)
        MI = same & (i <= c)
    BLa = np.broadcast_to(i < 64, (128, 128))
    BLb = np.broadcast_to(i >= 64, (128, 128))
    return np.ascontiguousarray(np.stack([MU, MGT, MI, BLa, BLb], axis=1).astype(np.float32))


def phase_gdscan(S, C, T, dirsel, reverse, QnT, KnT, Kn, Vt, BG, MK_d, S0_d, Send_d, O_out, fin=None, NSLOT=3):
    NU = T // 128
    MK = S.sb([128, 5, 128], F32)
    BMK = Buf()
    S.op("sp", lambda e: e.dma_start(out=MK[:], in_=MK_d), writes=[BMK], dma=True)
    Sf = S.sb([128, 8, 128], F32)
    Sb = S.sb([128, 8, 128], BF16)
    BSf = [Buf() for _ in range(8)]
    BSb = [Buf() for _ in range(8)]
    BS0 = Buf()
    S.op("sp", lambda e: e.dma_start(out=Sf[:], in_=S0_d.rearrange("h p d -> p h d")), writes=[BS0] + BSf, dma=True)
    for h in range(8):
        S.op("act", lambda e, h=h: e.activation(out=Sb[:, h, :], in_=Sf[:, h, :], func=AF.Copy), reads=[BSf[h]], writes=[BSb[h]])

    kTr = Rot(S, 2, [128, 8, 128], BF16)
    qTr = Rot(S, 2, [128, 8, 128], BF16)
    knr = Rot(S, 2, [128, D], BF16)
    vtr = Rot(S, 2, [128, D], BF16)
    bgr = Rot(S, 2, [128, 32], F32)
    smr = Rot(S, 2, [128, 48], F32)
    vbr = Rot(S, 2, [128, D], BF16)
    kgr = Rot(S, 2, [128, D], BF16)
    ttr = Rot(S, 2, [128, 8, 128], BF16)
    qkr = Rot(S, 2, [128, 8, 128], BF16)
    Or = Rot(S, 2, [128, D], F32)
    slots = []
    for s in range(NSLOT):
        slots.append(dict(
            ps=Rot(S, 2, [128, 512], F32, psum=True),
            f=Rot(S, 14, [128, 128], F32),
            b=Rot(S, 4, [128, 128], BF16),
        ))
    if fin is not None:
        wo, Bwo = load_w_bf16(S, fin["w_out"], 8, D)
        gnw = S.sb([128, 128], F32)
        Bgnw = Buf()
        S.op("sp", lambda e: e.dma_start(out=gnw[:], in_=fin["gnw"]), writes=[Bgnw], dma=True)
        oar = Rot(S, 2, [128, D], F32)
        szr = Rot(S, 2, [128, D], F32)
        xrr = Rot(S, 2, [128, D], F32)
        gbr = Rot(S, 2, [128, D], BF16)
        gTr = Rot(S, 2, [128, 1024], BF16)
        str_ = Rot(S, 2, [128, 16], F32)
        jkr = Rot(S, 1, [128, D], F32)
        pyr = Rot(S, 1, [128, 512], F32, psum=True)
        ptr1 = Rot(S, 1, [128, 1024], BF16, psum=True)

    order = list(range(NU)) if not reverse else list(range(NU - 1, -1, -1))
    chunks = (0, 64) if not reverse else (64, 0)
    gcol = 16 + 8 * dirsel
    bcol = 8 * dirsel
    loaded = {}

    def loads(u):
        u0 = u * 128
        kT, BkT = kTr.next()
        qT, BqT = qTr.next()
        kn, Bkn = knr.next()
        vt, Bvt = vtr.next()
        bg, Bbg = bgr.next()
        S.op("sp", lambda e: e.dma_start(out=kT[:], in_=KnT[:, :, u0:u0 + 128].rearrange("h p t -> p h t")), writes=[BkT], dma=True)
        S.op("sp", lambda e: e.dma_start(out=qT[:], in_=QnT[:, :, u0:u0 + 128].rearrange("h p t -> p h t")), writes=[BqT], dma=True)
        S.op("sp", lambda e: e.dma_start(out=kn[:], in_=Kn[u0:u0 + 128, :]), writes=[Bkn], dma=True)
        S.op("sp", lambda e: e.dma_start(out=vt[:], in_=Vt[u0:u0 + 128, :]), writes=[Bvt], dma=True)
        S.op("sp", lambda e: e.dma_start(out=bg[:], in_=BG[u0:u0 + 128, :]), writes=[Bbg], dma=True)
        d = dict(kT=kT, BkT=BkT, qT=qT, BqT=BqT, kn=kn, Bkn=Bkn, vt=vt, Bvt=Bvt, bg=bg, Bbg=Bbg)
        if fin is not None:
            oa, Boa = oar.next()
            sz, Bsz = szr.next()
            xr, Bxr = xrr.next()
            S.op("sp", lambda e: e.dma_start(out=oa[:], in_=fin["O_A"][u0:u0 + 128, :]), writes=[Boa], dma=True)
            S.op("sp", lambda e: e.dma_start(out=sz[:], in_=fin["SZ"][u0:u0 + 128, :]), writes=[Bsz], dma=True)
            S.op("sp", lambda e: e.dma_start(out=xr[:], in_=fin["x_in"][u0:u0 + 128, :]), writes=[Bxr], dma=True)
            d.update(oa=oa, Boa=Boa, sz=sz, Bsz=Bsz, xr=xr, Bxr=Bxr)
        loaded[u] = d

    def unit(u):
        L = loaded.pop(u)
        kT, BkT, qT, BqT, kn, Bkn, vt, Bvt, bg, Bbg = (L[k] for k in ("kT", "BkT", "qT", "BqT", "kn", "Bkn", "vt", "Bvt", "bg", "Bbg"))
        sm, Bsm = smr.next()
        ps0, Bps0 = slots[0]["ps"].next()
        for i, mi in enumerate((0, 1, 3, 4)):
            S.op("pe", lambda e, i=i, mi=mi: e.matmul(ps0[:, 8 * i:8 * i + 8], lhsT=MK[:, mi, :], rhs=bg[:, gcol:gcol + 8], start=True, stop=True),
                 reads=[BMK, Bbg], writes=[Bps0], acc=True)
        S.op("act", lambda e: e.activation(out=sm[:, 0:32], in_=ps0[:, 0:32], func=AF.Exp), reads=[Bps0], writes=[Bsm])
        S.op("dve", lambda e: e.scalar_tensor_tensor(out=sm[:, 32:40], in0=sm[:, 0:8], scalar=-1.0, in1=bg[:, bcol:bcol + 8],
                                                     op0=ALU.mult, op1=ALU.mult), reads=[Bsm, Bbg], writes=[Bsm])
        vb, Bvb = vbr.next()
        kg, Bkg = kgr.next()
        for h in range(8):
            S.op("pool", lambda e, h=h: e.tensor_scalar_mul(out=vb[:, h * 128:(h + 1) * 128], in0=vt[:, h * 128:(h + 1) * 128],
                                                            scalar1=bg[:, bcol + h:bcol + h + 1]), reads=[Bvt, Bbg], writes=[Bvb], acc=True)
        for h in range(8):
            S.op("pool", lambda e, h=h: e.tensor_scalar_mul(out=kg[:, h * 128:(h + 1) * 128], in0=kn[:, h * 128:(h + 1) * 128],
                                                            scalar1=sm[:, 8 + h:9 + h]), reads=[Bkn, Bsm], writes=[Bkg], acc=True)
        tt, Btt0 = ttr.next()
        qk, Bqk0 = qkr.next()
        Btt = [Buf() for _ in range(8)]
        Bqk = [Buf() for _ in range(8)]
        for h in range(8):
            Btt[h].w, Btt[h].r = Btt0.w, dict(Btt0.r)
            Bqk[h].w, Bqk[h].r = Bqk0.w, dict(Bqk0.r)
        O, BO0 = Or.next()
        BO = [Buf() for _ in range(8)]
        for h in range(8):
            BO[h].w, BO[h].r = BO0.w, dict(BO0.r)

        def prep_head(sl, h):
            PS, FR = sl["ps"], sl["f"]
            gp, Bgp = FR.next()
            S.op("dve", lambda e: e.tensor_scalar_mul(out=gp[:], in0=MK[:, 1, :], scalar1=bg[:, gcol + h:gcol + h + 1]), reads=[BMK, Bbg], writes=[Bgp])
            pd, Bpd = PS.next()
            S.op("pe", lambda e: e.matmul(pd[:, 0:128], lhsT=MK[:, 0, :], rhs=gp[:], start=True, stop=True), reads=[BMK, Bgp], writes=[Bpd])
            ed, Bed = FR.next()
            S.op("act", lambda e: e.activation(out=ed[:], in_=pd[:, 0:128], func=AF.Exp), reads=[Bpd], writes=[Bed])
            yield
            eds, Beds = FR.next()
            edi, Bedi = FR.next()
            S.op("pool", lambda e: e.tensor_tensor(out=eds[:], in0=ed[:], in1=MK[:, 1, :], op=ALU.mult), reads=[Bed, BMK], writes=[Beds])
            S.op("pool", lambda e: e.tensor_tensor(out=edi[:], in0=ed[:], in1=MK[:, 2, :], op=ALU.mult), reads=[Bed, BMK], writes=[Bedi])
            pk, Bpk = PS.next()
            S.op("pe", lambda e: e.matmul(pk[:, 0:128], lhsT=kT[:, h, :], rhs=kT[:, h, :], start=True, stop=True), reads=[BkT], writes=[Bpk])
            P, BP = FR.next()
            S.op("dve", lambda e: e.scalar_tensor_tensor(out=P[:], in0=pk[:, 0:128], scalar=bg[:, bcol + h:bcol + h + 1], in1=eds[:],
                                                         op0=ALU.mult, op1=ALU.mult), reads=[Bpk, Bbg, Beds], writes=[BP])
            yield
            pq, Bpq = PS.next()
            S.op("pe", lambda e: e.matmul(pq[:, 0:128], lhsT=qT[:, h, :], rhs=kT[:, h, :], start=True, stop=True), reads=[BqT, BkT], writes=[Bpq])
            qd, Bqd = FR.next()
            S.op("dve", lambda e: e.tensor_tensor(out=qd[:], in0=pq[:, 0:128], in1=edi[:], op=ALU.mult), reads=[Bpq, Bedi], writes=[Bqd])
            yield
            pl, Bpl = PS.next()
            S.op("pe", lambda e: e.transpose(out=pl[:, 0:128], in_=P[:], identity=C.idf[:]), reads=[BP, C.Bidf], writes=[Bpl])
            Pt, BPt = FR.next()
            S.op("act", lambda e: e.activation(out=Pt[:], in_=pl[:, 0:128], func=AF.Copy), reads=[Bpl], writes=[BPt])
            yield
            pq2, Bpq2 = PS.next()
            S.op("pe", lambda e: e.transpose(out=pq2[:, 0:128], in_=qd[:], identity=C.idf[:]), reads=[Bqd, C.Bidf], writes=[Bpq2])
            S.op("act", lambda e: e.activation(out=qk[:, h, :], in_=pq2[:, 0:128], func=AF.Copy), reads=[Bpq2], writes=[Bqk[h]])
            W, BW = FR.next()
            S.op("pool", lambda e: e.tensor_tensor(out=W[:], in0=C.idf[:], in1=Pt[:], op=ALU.subtract), reads=[C.Bidf, BPt], writes=[BW])
            yield
            for k in range(1, 6):
                pp, Bpp = PS.next()
                S.op("pe", lambda e: e.matmul(pp[:, 0:128], lhsT=Pt[:], rhs=P[:], start=True, stop=True), reads=[BPt, BP], writes=[Bpp])
                Pn, BPn = FR.next()
                S.op("act", lambda e: e.activation(out=Pn[:], in_=pp[:, 0:128], func=AF.Copy), reads=[Bpp], writes=[BPn])
                yield
                if k < 5:
                    pt_, Bpt_ = PS.next()
                    S.op("pe", lambda e: e.matmul(pt_[:, 0:128], lhsT=P[:], rhs=Pt[:], start=True, stop=True), reads=[BP, BPt], writes=[Bpt_])
                    Ptn, BPtn = FR.next()
                    S.op("dve", lambda e: e.tensor_copy(out=Ptn[:], in_=pt_[:, 0:128]), reads=[Bpt_], writes=[BPtn])
                    yield
                pw, Bpw = PS.next()
                S.op("pe", lambda e: e.matmul(pw[:, 0:128], lhsT=Pn[:], rhs=W[:], start=True, stop=True), reads=[BPn, BW], writes=[Bpw])
                if k < 5:
                    Wn, BWn = FR.next()
                    S.op("dve", lambda e: e.tensor_tensor(out=Wn[:], in0=pw[:, 0:128], in1=W[:], op=ALU.add), reads=[Bpw, BW], writes=[BWn])
                    W, BW = Wn, BWn
                    P, BP, Pt, BPt = Pn, BPn, Ptn, BPtn
                else:
                    S.op("dve", lambda e: e.tensor_tensor(out=tt[:, h, :], in0=pw[:, 0:128], in1=W[:], op=ALU.add), reads=[Bpw, BW], writes=[Btt[h]])
                yield

        def slot_prep(si):
            for h in range(si, 8, NSLOT):
                yield from prep_head(slots[si], h)

        run_rr([slot_prep(si) for si in range(NSLOT)])

        def scan_head(sl, h, r0):
            PS, FR, BR = sl["ps"], sl["f"], sl["b"]
            rows = slice(r0, r0 + 64)
            hs = slice(h * 128, (h + 1) * 128)
            pa_, Bpa = PS.next()
            S.op("pe", lambda e: e.matmul(pa_[:, 0:128], lhsT=kT[:, h, :], rhs=Sb[:, h, :], start=True, stop=True), reads=[BkT, BSb[h]], writes=[Bpa])
            R, BR_ = BR.next()
            S.op("dve", lambda e: e.scalar_tensor_tensor(out=R[rows, :], in0=pa_[rows, 0:128], scalar=sm[rows, 32 + h:33 + h], in1=vb[rows, hs],
                                                         op0=ALU.mult, op1=ALU.add), reads=[Bpa, Bsm, Bvb], writes=[BR_])
            yield
            pv_, Bpv = PS.next()
            S.op("pe", lambda e: e.matmul(pv_[:, 0:128], lhsT=tt[rows, h, :], rhs=R[rows, :], start=True, stop=True), reads=[Btt[h], BR_], writes=[Bpv])
            vn, Bvn = BR.next()
            S.op("act", lambda e: e.activation(out=vn[rows, :], in_=pv_[rows, 0:128], func=AF.Copy), reads=[Bpv], writes=[Bvn])
            yield
            po1, Bpo1 = PS.next()
            S.op("pe", lambda e: e.matmul(po1[:, 0:128], lhsT=qT[:, h, :], rhs=Sb[:, h, :], start=True, stop=True), reads=[BqT, BSb[h]], writes=[Bpo1])
            t1, Bt1 = FR.next()
            S.op("act", lambda e: e.activation(out=t1[rows, :], in_=po1[rows, 0:128], func=AF.Identity, scale=sm[rows, h:h + 1]),
                 reads=[Bpo1, Bsm], writes=[Bt1])
            yield
            po2, Bpo2 = PS.next()
            S.op("pe", lambda e: e.matmul(po2[:, 0:128], lhsT=qk[rows, h, :], rhs=vn[rows, :], start=True, stop=True), reads=[Bqk[h], Bvn], writes=[Bpo2])
            S.op("dve", lambda e: e.tensor_tensor(out=O[rows, hs], in0=po2[rows, 0:128], in1=t1[rows, :], op=ALU.add), reads=[Bpo2, Bt1], writes=[BO[h]])
            yield
            pds, Bpds = PS.next()
            S.op("pe", lambda e: e.matmul(pds[:, 0:128], lhsT=kg[rows, hs], rhs=vn[rows, :], start=True, stop=True), reads=[Bkg, Bvn], writes=[Bpds])
            gl = 16 if r0 == 0 else 24
            S.op("dve", lambda e: e.scalar_tensor_tensor(out=Sf[:, h, :], in0=Sf[:, h, :], scalar=sm[:, gl + h:gl + h + 1], in1=pds[:, 0:128],
                                                         op0=ALU.mult, op1=ALU.add), reads=[Bpds, Bsm, BSf[h]], writes=[BSf[h]])
            S.op("act", lambda e: e.activation(out=Sb[:, h, :], in_=Sf[:, h, :], func=AF.Copy), reads=[BSf[h]], writes=[BSb[h]])
            yield

        def slot_scan(si, r0):
            for h in range(si, 8, NSLOT):
                yield from scan_head(slots[si], h, r0)

        for r0 in chunks:
            run_rr([slot_scan(si, r0) for si in range(NSLOT)])

        u0 = u * 128
        if fin is None:
            S.op("sp", lambda e: e.dma_start(out=O_out[u0:u0 + 128, :], in_=O[:]), reads=BO, writes=[BO0], dma=True, is_out=True)
        else:
            oa, Boa, sz, Bsz, xr, Bxr = (L[k] for k in ("oa", "Boa", "sz", "Bsz", "xr", "Bxr"))
            S.op("dve", lambda e: e.tensor_tensor(out=oa[:], in0=oa[:], in1=O[:], op=ALU.add), reads=BO + [Boa], writes=[Boa, BO0])
            jk, Bjk = jkr.next()
            st, Bst = str_.next()
            S.op("act", lambda e: e.activation(out=jk[:], in_=oa[:], func=AF.Square), reads=[Boa], writes=[Bjk])
            S.op("dve", lambda e: e.reduce_sum(out=st[:, 0:8], in_=jk[:].rearrange("p (h d) -> p h d", h=8), axis=AX.X), reads=[Bjk], writes=[Bst])
            S.op("act", lambda e: e.activation(out=st[:, 8:16], in_=st[:, 0:8], func=AF.Sqrt, bias=EPS, scale=1.0 / 128), reads=[Bst], writes=[Bst])
            S.op("dve", lambda e: e.reciprocal(out=st[:, 8:16], in_=st[:, 8:16]), reads=[Bst], writes=[Bst])
            for h in range(8):
                S.op("dve", lambda e, h=h: e.scalar_tensor_tensor(out=oa[:, h * 128:(h + 1) * 128], in0=oa[:, h * 128:(h + 1) * 128],
                                                                  scalar=st[:, 8 + h:9 + h], in1=gnw[:], op0=ALU.mult, op1=ALU.mult),
                     reads=[Boa, Bst, Bgnw], writes=[Boa])
            gb, Bgb = gbr.next()
            S.op("pool", lambda e: e.tensor_tensor(out=gb[:], in0=oa[:], in1=sz[:], op=ALU.mult), reads=[Boa, Bsz], writes=[Bgb])
            ptr, Bptr = ptr1.next()
            for h in range(8):
                S.op("pe", lambda e, h=h: e.transpose(out=ptr[:, h * 128:(h + 1) * 128], in_=gb[:, h * 128:(h + 1) * 128], identity=C.idb[:]),
                     reads=[Bgb, C.Bidb], writes=[Bptr], acc=True)
            gT, BgT = gTr.next()
            S.op("act", lambda e: e.activation(out=gT[:, :], in_=ptr[:, :], func=AF.Copy), reads=[Bptr], writes=[BgT])
            for dh in range(2):
                y_ps, By = pyr.next()
                for h in range(8):
                    S.op("pe", lambda e, h=h, dh=dh: e.matmul(y_ps[:, :], lhsT=gT[:, h * 128:(h + 1) * 128], rhs=wo[:, h, dh * 512:(dh + 1) * 512],
                                                              start=(h == 0), stop=(h == 7)), reads=[BgT, Bwo[h]], writes=[By], acc=True)
                S.op("dve", lambda e, dh=dh: e.tensor_tensor(out=xr[:, dh * 512:(dh + 1) * 512], in0=y_ps[:], in1=xr[:, dh * 512:(dh + 1) * 512],
                                                             op=ALU.add), reads=[By, Bxr], writes=[Bxr])
            S.op("sp", lambda e: e.dma_start(out=fin["x_out"][u0:u0 + 128, :], in_=xr[:]), reads=[Bxr], dma=True, is_out=True)
        for B0, Bl in ((Btt0, Btt), (Bqk0, Bqk)):
            for b in Bl:
                for kx, tk in b.r.items():
                    B0.r[(kx, id(b))] = tk
                if b.w is not None:
                    B0.r[("w", id(b))] = b.w

    loads(order[0])
    for i, u in enumerate(order):
        if i + 1 < len(order):
            loads(order[i + 1])
        unit(u)
    S.op("sp", lambda e: e.dma_start(out=Send_d.rearrange("h p d -> p h d"), in_=Sf[:]), reads=BSf, dma=True, is_out=True)


def build_gdin(T):
    nc = bass.Bass("TRN2", target_bir_lowering=False)
    x_in = nc.dram_tensor("x", [T, D], F32, kind="ExternalInput").ap()
    xh = nc.dram_tensor("xh", [16, D], F32, kind="ExternalInput").ap()
    w_in = nc.dram_tensor("w_in", [D, 4128], F32, kind="ExternalInput").ap()
    nwT = nc.dram_tensor("nwT", [128, 8], F32, kind="ExternalInput").ap()
    ab = nc.dram_tensor("ab", [128, 32], F32, kind="ExternalInput").ap()
    ident = nc.dram_tensor("ident", [128, 128], F32, kind="ExternalInput").ap()
    PT = nc.dram_tensor("PT", [24, 128, T + 16], F32, kind="ExternalOutput").ap()
    SZ = nc.dram_tensor("SZ", [T, D], F32, kind="ExternalOutput").ap()
    BG = nc.dram_tensor("BG", [T, 32], F32, kind="ExternalOutput").ap()
    S = Sched(nc)
    C = Common(S, ident)
    phase_gdin(S, C, T, x_in, xh, w_in, nwT, ab, PT, SZ, BG)
    S.emit()
    return nc


def build_gdprep(T):
    nc = bass.Bass("TRN2", target_bir_lowering=False)
    PT = nc.dram_tensor("PT", [24, 128, T + 16], F32, kind="ExternalInput").ap()
    cw = nc.dram_tensor("cw", [128, 24, 5], F32, kind="ExternalInput").ap()
    ident = nc.dram_tensor("ident", [128, 128], F32, kind="ExternalInput").ap()
    QnT = nc.dram_tensor("QnT", [8, 128, T], BF16, kind="ExternalOutput").ap()
    KnT = nc.dram_tensor("KnT", [8, 128, T], BF16, kind="ExternalOutput").ap()
    Kn = nc.dram_tensor("Kn", [T, D], BF16, kind="ExternalOutput").ap()
    Vt = nc.dram_tensor("Vt", [T, D], BF16, kind="ExternalOutput").ap()
    S = Sched(nc)
    C = Common(S, ident)
    phase_gdprep(S, C, T, PT, cw, QnT, KnT, Kn, Vt)
    S.emit()
    return nc


def build_gdscan(T, second, dbg=False):
    nc = bass.Bass("TRN2", target_bir_lowering=False)
    QnT = nc.dram_tensor("QnT", [8, 128, T], BF16, kind="ExternalInput").ap()
    KnT = nc.dram_tensor("KnT", [8, 128, T], BF16, kind="ExternalInput").ap()
    Kn = nc.dram_tensor("Kn", [T, D], BF16, kind="ExternalInput").ap()
    Vt = nc.dram_tensor("Vt", [T, D], BF16, kind="ExternalInput").ap()
    BG = nc.dram_tensor("BG", [T, 32], F32, kind="ExternalInput").ap()
    MK = nc.dram_tensor("MK", [128, 5, 128], F32, kind="ExternalInput").ap()
    S0 = nc.dram_tensor("S0", [8, 128, 128], F32, kind="ExternalInput").ap()
    ident = nc.dram_tensor("ident", [128, 128], F32, kind="ExternalInput").ap()
    Send = nc.dram_tensor("Send", [8, 128, 128], F32, kind="ExternalOutput").ap()
    S = Sched(nc)
    C = Common(S, ident)
    if dbg:
        O = nc.dram_tensor("O", [T, D], F32, kind="ExternalOutput").ap()
        phase_gdscan(S, C, T, 1, True, QnT, KnT, Kn, Vt, BG, MK, S0, Send, O)
    elif not second:
        O = nc.dram_tensor("O", [T, D], F32, kind="ExternalOutput").ap()
        phase_gdscan(S, C, T, 0, False, QnT, KnT, Kn, Vt, BG, MK, S0, Send, O)
    else:
        fin = dict(
            O_A=nc.dram_tensor("O_A", [T, D], F32, kind="ExternalInput").ap(),
            SZ=nc.dram_tensor("SZ", [T, D], F32, kind="ExternalInput").ap(),
            gnw=nc.dram_tensor("gnw", [128, 128], F32, kind="ExternalInput").ap(),
            w_out=nc.dram_tensor("w_out", [D, D], F32, kind="ExternalInput").ap(),
            x_in=nc.dram_tensor("x", [T, D], F32, kind="ExternalInput").ap(),
            x_out=nc.dram_tensor("y", [T, D], F32, kind="ExternalOutput").ap(),
        )
        phase_gdscan(S, C, T, 1, True, QnT, KnT, Kn, Vt, BG, MK, S0, Send, None, fin=fin)
    S.emit()
    return nc


NCORES = 8
TCORE = 4096
_NC_CACHE = {}


def _lambda_init(layer_idx):
    import math
    return 0.8 - 0.6 * math.exp(-0.3 * layer_idx)


def _get_nc(key, builder):
    if key not in _NC_CACHE:
        _NC_CACHE[key] = builder()
    return _NC_CACHE[key]


def _launch(key, builder, maps):
    nc = _get_nc(key, builder)
    res = run_bass_kernel_spmd(nc, maps, core_ids=list(range(len(maps))))
    return res.results


def _halo(xs):
    T = xs[0].shape[0]
    return [np.ascontiguousarray(xs[c ^ 1][T - 16:][::-1]) for c in range(len(xs))]


def kernel(x, rel_bias, norm_w, final_norm_w, ffn_w_in, ffn_w_out, ac_w_in, ac_w_out, da_lambda, da_subln_w,
           conv_w, conv_b, conv_norm_w, conv_norm_b, gd_w_in, gd_w_out, gd_conv_w, gd_a_log, gd_dt_bias, gd_norm_w):
    f = np.float32
    x = np.asarray(x, f)
    B, SEQ, _ = x.shape
    T = SEQ // 2
    n = 2 * B
    ident = np.eye(128, dtype=f)
    xs = []
    for b in range(B):
        xs.append(np.ascontiguousarray(x[b, :T]))
        xs.append(np.ascontiguousarray(x[b, T:][::-1]))
    depth = norm_w.shape[0]

    def ffn(xs, l, i, final):
        maps = []
        for c in range(n):
            m = {"x": xs[c], "w_in": np.asarray(ffn_w_in[l, i], f), "w_out": np.asarray(ffn_w_out[l, i], f),
                 "nwT": colT(norm_w[l, 0 if i == 0 else 2]), "ident": ident}
            if final:
                m["fin"] = np.ascontiguousarray(np.broadcast_to(np.asarray(final_norm_w, f), (128, D)))
            maps.append(m)
        r = _launch(("ffn", T, final), lambda: build_ffn(T, final), maps)
        return [r[c]["y"] for c in range(n)]

    def ac_layer(xs, l):
        j = l // 2
        xh = _halo(xs)
        maps = [{"x": xs[c], "xh": xh[c], "w_in": np.asarray(ac_w_in[j], f), "nwT": colT(norm_w[l, 1]), "ident": ident} for c in range(n)]
        r1 = _launch(("acin", T), lambda: build_acin(T), maps)
        lam_init = _lambda_init(l)
        bt = [make_bias_tiles(np.asarray(rel_bias, f), p) for p in range(2)]
        cbs = [make_cb(np.asarray(rel_bias, f), p) for p in range(2)]
        lamb = np.ascontiguousarray(np.broadcast_to(np.asarray(da_lambda[j], f).reshape(256), (128, 256)))
        sw = np.ascontiguousarray(np.asarray(da_subln_w[j], f)[:, None])
        maps = []
        for c in range(n):
            maps.append({"QT": r1[c]["QT"], "KTf": np.concatenate([r1[c]["KT"], r1[c ^ 1]["KT"]], axis=2),
                         "Vf": np.concatenate([r1[c]["V"], r1[c ^ 1]["V"]], axis=0), "BT": bt[c % 2], "cb": cbs[c % 2],
                         "lamb": lamb, "sw": sw, "ident": ident})
        r2 = _launch(("att", T, l), lambda: build_att(T, lam_init), maps)
        cv = np.ascontiguousarray(np.stack([colT(conv_b[j]), colT(conv_norm_w[j]), colT(conv_norm_b[j])], axis=1))
        maps = []
        for c in range(n):
            w = np.asarray(conv_w[j], f)
            if c % 2 == 1:
                w = w[::-1]
            cw = np.ascontiguousarray(w.T.reshape(8, 128, 31).transpose(1, 0, 2))
            maps.append({"UT": r1[c]["UT"], "cw": cw, "cv": cv, "OAT": r2[c]["OAT"], "w_out": np.asarray(ac_w_out[j], f),
                         "x": xs[c], "ident": ident})
        r3 = _launch(("convout", T), lambda: build_convout(T), maps)
        return [r3[c]["y"] for c in range(n)]

    def gd_layer(xs, l):
        j = l // 2
        xh = _halo(xs)
        w0 = np.asarray(gd_w_in[j], f)
        maps = []
        for c in range(n):
            d = [0, 1] if c % 2 == 0 else [1, 0]
            w = w0.copy()
            w[:, 4096:4104] = w0[:, 4096 + 8 * d[0]:4104 + 8 * d[0]]
            w[:, 4104:4112] = w0[:, 4096 + 8 * d[1]:4104 + 8 * d[1]]
            w[:, 4112:4120] = w0[:, 4112 + 8 * d[0]:4120 + 8 * d[0]]
            w[:, 4120:4128] = w0[:, 4112 + 8 * d[1]:4120 + 8 * d[1]]
            ab = np.concatenate([gd_a_log[j][d[0]], gd_a_log[j][d[1]], gd_dt_bias[j][d[0]], gd_dt_bias[j][d[1]]]).astype(f)
            maps.append({"x": xs[c], "xh": xh[c], "w_in": w, "nwT": colT(norm_w[l, 1]),
                         "ab": np.ascontiguousarray(np.broadcast_to(ab, (128, 32))), "ident": ident})
        r1 = _launch(("gdin", T), lambda: build_gdin(T), maps)
        maps = []
        for c in range(n):
            w = np.asarray(gd_conv_w[j], f)
            if c % 2 == 1:
                w = w[::-1]
            cw = np.ascontiguousarray(w.T.reshape(24, 128, 5).transpose(1, 0, 2))
            maps.append({"PT": r1[c]["PT"], "cw": cw, "ident": ident})
        r2 = _launch(("gdprep", T), lambda: build_gdprep(T), maps)
        mkf, mkr = gd_masks(False), gd_masks(True)
        z0 = np.zeros((8, 128, 128), f)
        maps = [{"QnT": r2[c]["QnT"], "KnT": r2[c]["KnT"], "Kn": r2[c]["Kn"], "Vt": r2[c]["Vt"], "BG": r1[c]["BG"],
                 "MK": mkf, "S0": z0, "ident": ident} for c in range(n)]
        r3 = _launch(("scanA", T), lambda: build_gdscan(T, False), maps)
        gnw = np.ascontiguousarray(np.broadcast_to(np.asarray(gd_norm_w[j], f), (128, 128)))
        maps = [{"QnT": r2[c]["QnT"], "KnT": r2[c]["KnT"], "Kn": r2[c]["Kn"], "Vt": r2[c]["Vt"], "BG": r1[c]["BG"],
                 "MK": mkr, "S0": r3[c ^ 1]["Send"], "ident": ident, "O_A": r3[c]["O"], "SZ": r1[c]["SZ"], "gnw": gnw,
                 "w_out": np.asarray(gd_w_out[j], f), "x": xs[c]} for c in range(n)]
        r4 = _launch(("scanB", T), lambda: build_gdscan(T, True), maps)
        return [r4[c]["y"] for c in range(n)]

    for l in range(depth):
        xs = ffn(xs, l, 0, False)
        xs = ac_layer(xs, l) if l % 2 == 0 else gd_layer(xs, l)
        xs = ffn(xs, l, 1, l == depth - 1)
    out = np.empty((B, SEQ, D), f)
    for b in range(B):
        out[b, :T] = xs[2 * b]
        out[b, T:] = xs[2 * b + 1][::-1]
    return out
```
